# Optimizing a Trainium2 kernel written in Bass

```python
import math
import jax, jax.numpy as jnp
from jax import lax

D_MODEL = 2048
BATCH = 2
SEQ = 4096
DEPTH = 4

N_META = 16
N_A_LAYERS = DEPTH // 2
N_B_LAYERS = DEPTH - N_A_LAYERS
D_RNN = 5 * D_MODEL // 4
N_GATE_BLOCKS = 16
GATE_BLOCK = D_RNN // N_GATE_BLOCKS
CONV_WIDTH = 4
LRU_C = 8.0
HEAD_DIM = 128
N_DIFF_HEADS = D_MODEL // (2 * HEAD_DIM)
QK_WIDTH = 2 * N_DIFF_HEADS * HEAD_DIM
V_WIDTH = N_DIFF_HEADS * 2 * HEAD_DIM
D_FF = 4 * D_MODEL
ROPE_THETA = 10000.0
Q_BLOCK = 128
NORM_EPS = 1e-6
SUBLN_EPS = 1e-5

kernel_name = "yoco_rglru_diffattn_hybrid"


def rmsnorm(x, g, eps=NORM_EPS):
    xf = x.astype(jnp.float32)
    xf = xf * lax.rsqrt(jnp.mean(xf * xf, axis=-1, keepdims=True) + eps)
    return (xf * g.astype(jnp.float32)).astype(x.dtype)


def rope_tables(length):
    inv = 1.0 / (ROPE_THETA ** (jnp.arange(0, HEAD_DIM, 2, dtype=jnp.float32) / HEAD_DIM))
    ang = jnp.arange(length, dtype=jnp.float32)[:, None] * inv[None, :]
    return jnp.cos(ang), jnp.sin(ang)


def apply_rope(x, cos, sin):
    c = cos[None, :, None, None, :].astype(x.dtype)
    s = sin[None, :, None, None, :].astype(x.dtype)
    half = HEAD_DIM // 2
    x1, x2 = x[..., :half], x[..., half:]
    return jnp.concatenate([x1 * c - x2 * s, x2 * c + x1 * s], axis=-1)


def causal_depthwise_conv(x, w, b):
    y = lax.conv_general_dilated(
        x, w[:, None, :].astype(x.dtype), window_strides=(1,),
        padding=[(CONV_WIDTH - 1, 0)], dimension_numbers=('NWC', 'WIO', 'NWC'),
        feature_group_count=x.shape[-1])
    return y + b.astype(x.dtype)


def _linear_combine(left, right):
    a_l, b_l = left
    a_r, b_r = right
    return a_l * a_r, a_r * b_l + b_r


def rglru_block(u, w_in, conv_w, conv_b, w_r, b_r, w_i, b_i, lam, w_out):
    B, L, _ = u.shape
    proj = u @ w_in
    gate = jax.nn.gelu(proj[..., :D_RNN])
    xr = causal_depthwise_conv(proj[..., D_RNN:], conv_w, conv_b)
    xb = xr.reshape(B, L, N_GATE_BLOCKS, GATE_BLOCK)
    r = jax.nn.sigmoid(jnp.einsum('blnc,ncd->blnd', xb, w_r).reshape(B, L, D_RNN).astype(jnp.float32)
                       + b_r.astype(jnp.float32))
    i = jax.nn.sigmoid(jnp.einsum('blnc,ncd->blnd', xb, w_i).reshape(B, L, D_RNN).astype(jnp.float32)
                       + b_i.astype(jnp.float32))
    log_a = -LRU_C * r * jax.nn.softplus(-lam.astype(jnp.float32))
    a = jnp.exp(log_a)
    bvals = jnp.sqrt(-jnp.expm1(2.0 * log_a)) * (i * xr.astype(jnp.float32))
    _, hs = lax.associative_scan(_linear_combine, (a, bvals), axis=1)
    y = hs.astype(u.dtype) * gate
    return y @ w_out


def squared_relu_mlp(u, w1, w2):
    return jnp.square(jax.nn.relu(u @ w1)) @ w2


def shared_kv(h, g, w_kv, cos, sin):
    B, L, _ = h.shape
    kv = rmsnorm(h, g) @ w_kv
    k = apply_rope(kv[..., :QK_WIDTH].reshape(B, L, N_DIFF_HEADS, 2, HEAD_DIM), cos, sin)
    v = kv[..., QK_WIDTH:].reshape(B, L, N_DIFF_HEADS, 2 * HEAD_DIM)
    return k, v


def causal_diff_attention(q, k, v, lam):
    L = q.shape[1]
    scale = 1.0 / math.sqrt(HEAD_DIM)
    bounds = [0] + list(range(N_META, L, Q_BLOCK)) + [L]
    outs = []
    for s, e in zip(bounds[:-1], bounds[1:]):
        scores = jnp.einsum('bqhcd,bkhcd->bhcqk', q[:, s:e], k[:, :e],
                            preferred_element_type=jnp.float32) * scale
        mask = jnp.arange(e)[None, :] <= (s + jnp.arange(e - s))[:, None]
        p = jax.nn.softmax(jnp.where(mask, scores, -jnp.inf), axis=-1)
        attn = p[:, :, 0] - lam * p[:, :, 1]
        outs.append(jnp.einsum('bhqk,bkhe->bqhe', attn.astype(v.dtype), v[:, :e]))
    return jnp.concatenate(outs, axis=1)


def diff_attention_layer(u, k, v, w_q, lam_vecs, subln_g, w_o, cos, sin, lambda_init):
    B, L, _ = u.shape
    q = apply_rope((u @ w_q).reshape(B, L, N_DIFF_HEADS, 2, HEAD_DIM), cos, sin)
    lv = lam_vecs.astype(jnp.float32)
    lam = jnp.exp(jnp.sum(lv[0] * lv[1])) - jnp.exp(jnp.sum(lv[2] * lv[3])) + lambda_init
    o = causal_diff_attention(q, k, v, lam)
    o = rmsnorm(o, subln_g, SUBLN_EPS) * (1.0 - lambda_init)
    return o.reshape(B, L, V_WIDTH) @ w_o


def setup_inputs(seed: int = 0) -> dict:
    key = jax.random.key(seed)
    ks = jax.random.split(key, 24)
    f32 = jnp.float32

    def nrm(k, shape, fan_in):
        return jax.random.normal(k, shape, f32) * (fan_in ** -0.5)

    def gain(k, shape):
        return 1.0 + 0.05 * jax.random.normal(k, shape, f32)

    u = jax.random.uniform(ks[10], (N_A_LAYERS, D_RNN), f32, minval=0.9, maxval=0.999)
    a0 = u ** (1.0 / LRU_C)
    a_lambda = jnp.log(a0) - jnp.log1p(-a0)
    return {
        "x": jax.random.normal(ks[0], (BATCH, SEQ, D_MODEL), f32),
        "meta_tokens": jax.random.normal(ks[1], (N_META, D_MODEL), f32),
        "a_norm_g": gain(ks[2], (N_A_LAYERS, D_MODEL)),
        "a_w_in": nrm(ks[3], (N_A_LAYERS, D_MODEL, 2 * D_RNN), D_MODEL),
        "a_conv_w": nrm(ks[4], (N_A_LAYERS, CONV_WIDTH, D_RNN), CONV_WIDTH),
        "a_conv_b": 0.02 * jax.random.normal(ks[5], (N_A_LAYERS, D_RNN), f32),
        "a_w_r": nrm(ks[6], (N_A_LAYERS, N_GATE_BLOCKS, GATE_BLOCK, GATE_BLOCK), GATE_BLOCK),
        "a_b_r": 0.02 * jax.random.normal(ks[7], (N_A_LAYERS, D_RNN), f32),
        "a_w_i": nrm(ks[8], (N_A_LAYERS, N_GATE_BLOCKS, GATE_BLOCK, GATE_BLOCK), GATE_BLOCK),
        "a_b_i": 0.02 * jax.random.normal(ks[9], (N_A_LAYERS, D_RNN), f32),
        "a_lambda": a_lambda,
        "a_w_out": nrm(ks[11], (N_A_LAYERS, D_RNN, D_MODEL), D_RNN),
        "kv_norm_g": gain(ks[12], (D_MODEL,)),
        "w_kv": nrm(ks[13], (D_MODEL, QK_WIDTH + V_WIDTH), D_MODEL),
        "b_norm_g": gain(ks[14], (N_B_LAYERS, D_MODEL)),
        "b_w_q": nrm(ks[15], (N_B_LAYERS, D_MODEL, QK_WIDTH), D_MODEL),
        "b_lambda": 0.1 * jax.random.normal(ks[16], (N_B_LAYERS, 4, HEAD_DIM), f32),
        "b_subln_g": gain(ks[17], (N_B_LAYERS, 2 * HEAD_DIM)),
        "b_w_o": nrm(ks[18], (N_B_LAYERS, V_WIDTH, D_MODEL), V_WIDTH),
        "mlp_norm_g": gain(ks[19], (DEPTH, D_MODEL)),
        "mlp_w1": nrm(ks[20], (DEPTH, D_MODEL, D_FF), D_MODEL),
        "mlp_w2": nrm(ks[21], (DEPTH, D_FF, D_MODEL), D_FF),
        "final_norm_g": gain(ks[22], (D_MODEL,)),
    }


def reference(x, meta_tokens, a_norm_g, a_w_in, a_conv_w, a_conv_b, a_w_r, a_b_r, a_w_i, a_b_i,
              a_lambda, a_w_out, kv_norm_g, w_kv, b_norm_g, b_w_q, b_lambda, b_subln_g, b_w_o,
              mlp_norm_g, mlp_w1, mlp_w2, final_norm_g):
    B = x.shape[0]
    meta = jnp.broadcast_to(meta_tokens[None].astype(x.dtype), (B, N_META, D_MODEL))
    h = jnp.concatenate([meta, x], axis=1)
    cos, sin = rope_tables(h.shape[1])
    k_sh, v_sh = None, None
    for layer in range(DEPTH):
        if layer < N_A_LAYERS:
            j = layer
            h = h + rglru_block(rmsnorm(h, a_norm_g[j]), a_w_in[j], a_conv_w[j], a_conv_b[j],
                                a_w_r[j], a_b_r[j], a_w_i[j], a_b_i[j], a_lambda[j], a_w_out[j])
        else:
            j = layer - N_A_LAYERS
            if j == 0:
                k_sh, v_sh = shared_kv(h, kv_norm_g, w_kv, cos, sin)
            lambda_init = 0.8 - 0.6 * math.exp(-0.3 * layer)
            h = h + diff_attention_layer(rmsnorm(h, b_norm_g[j]), k_sh, v_sh, b_w_q[j], b_lambda[j],
                                         b_subln_g[j], b_w_o[j], cos, sin, lambda_init)
        h = h + squared_relu_mlp(rmsnorm(h, mlp_norm_g[layer]), mlp_w1[layer], mlp_w2[layer])
    h = rmsnorm(h, final_norm_g)
    return h[:, N_META:]
```

```python
import math
from contextlib import ExitStack

import numpy as np
import concourse.bass as bass
import concourse.mybir as mybir
from concourse.bass_utils import run_bass_kernel_spmd

F32 = mybir.dt.float32
BF16 = mybir.dt.bfloat16
AF = mybir.ActivationFunctionType
ALU = mybir.AluOpType

D = 2048
KC = 16
NMETA = 16
OWN = 1024
TT = 1043
C_OWN = 19
TILES = [(0, 19), (19, 531), (531, 1043)]
DRNN = 2560
RC = 20
DFF = 8192
NEG = -30000.0
SCALE = 1.0 / math.sqrt(128.0)
GELU = AF.Gelu_apprx_tanh

GT = [(0, 0), (0, 1), (1, 0), (1, 1), (1, 2), (2, 1), (2, 2), (2, 3), (3, 2), (3, 3), (3, 4), (4, 3), (4, 4)]
GT_IDX = {t: i for i, t in enumerate(GT)}


class Prog:
    CE = ("pe", "act", "dve", "pool")

    def __init__(self):
        self.ops = []
        self.lastw = {}
        self.readers = {}
        self.dma_n = {}

    def add(self, eng, fn, reads=(), writes=(), dma=None, inc=16):
        idx = len(self.ops)
        deps = set()
        for r in reads:
            w = self.lastw.get(r)
            if w is not None:
                deps.add(w)
        for k in writes:
            w = self.lastw.get(k)
            if w is not None:
                deps.add(w)
            rd = self.readers.get(k)
            if rd:
                deps.update(rd.values())
        if fn is not None:
            for r in reads:
                self.readers.setdefault(r, {})[(eng if dma is None else ("d", idx))] = idx
            for k in writes:
                self.lastw[k] = idx
                self.readers[k] = {}
        op = {"eng": eng, "fn": fn, "deps": deps, "dma": dma, "inc": inc, "sig": False}
        if dma is not None:
            n = self.dma_n.get(dma, 0) + 1
            self.dma_n[dma] = n
            op["dn"] = n
        self.ops.append(op)
        return idx

    def emit(self, nc, es):
        ops = self.ops
        for op in ops:
            for d in op["deps"]:
                dop = ops[d]
                if dop["dma"] is None:
                    if dop["eng"] == "pe" and op["eng"] == "pe" and op["dma"] is None:
                        continue
                    dop["sig"] = True
        cnt = {e: 0 for e in ("pe", "act", "dve", "pool", "sp")}
        for op in ops:
            if op["dma"] is None and op["sig"]:
                cnt[op["eng"]] += 1
                op["cnt"] = cnt[op["eng"]]
        esem = {e: es.enter_context(nc.semaphore("e_" + e)) for e in ("pe", "act", "dve", "pool", "sp")}
        dsem = {}
        for k in self.dma_n:
            dsem[k] = es.enter_context(nc.semaphore("d_%d" % len(dsem)))
        self.n_sems = len(esem) + len(dsem)
        block = es.enter_context(nc.Block())

        def run(ename, eng):
            waited = {}
            for op in ops:
                if op["eng"] != ename:
                    continue
                need = {}
                for d in op["deps"]:
                    dop = ops[d]
                    if dop["dma"] is not None:
                        s = dsem[dop["dma"]]
                        v = dop["inc"] * dop["dn"]
                    else:
                        if dop["eng"] == "pe" and ename == "pe" and op["dma"] is None:
                            continue
                        s = esem[dop["eng"]]
                        v = dop["cnt"]
                    key = id(s)
                    if v > need.get(key, (None, 0))[1]:
                        need[key] = (s, v)
                for key, (s, v) in need.items():
                    if waited.get(key, 0) >= v:
                        continue
                    eng.wait_ge(s, v)
                    waited[key] = v
                if op["fn"] is None:
                    continue
                ins = op["fn"](eng)
                if op["dma"] is not None:
                    ins.then_inc(dsem[op["dma"]], op["inc"])
                elif op["sig"]:
                    ins.then_inc(esem[ename], 1)

        @block.tensor
        def _(e):
            run("pe", e)

        @block.scalar
        def _(e):
            run("act", e)

        @block.vector
        def _(e):
            run("dve", e)

        @block.gpsimd
        def _(e):
            run("pool", e)

        @block.sync
        def _(e):
            run("sp", e)


class Builder:
    def __init__(self, stages, first, last):
        self.stages = stages
        self.first = first
        self.last = last
        self.P = Prog()
        self.nc = bass.Bass("TRN2", target_bir_lowering=False)
        self.es = ExitStack()
        self.wslot = 0
        self.pb = 0
        self.uid = 0
        self.declared = set()
        self.sq_i = 0
        self.sqr_i = 0

    def din(self, name, shape, dt=F32):
        self.declared.add(name)
        return self.nc.dram_tensor(name, list(shape), dt, kind="ExternalInput").ap()

    def D(self, name):
        if name not in self.dcache:
            self.dcache[name] = self.din(name, self.dshapes[name])
        return self.dcache[name]

    def dout(self, name, shape, dt=F32):
        return self.nc.dram_tensor(name, list(shape), dt, kind="ExternalOutput").ap()

    def sb(self, name, shape, dt):
        return self.es.enter_context(self.nc.sbuf_tensor("s_" + name, list(shape), dt))

    def tkeys(self, name, c, c0, c1):
        ks = []
        for ti, (a, b) in enumerate(TILES):
            if c0 < b and c1 > a:
                ks.append((name, c, ti))
        return ks

    def bank(self):
        b = self.pb
        self.pb = (self.pb + 1) % 8
        return b

    def op(self, eng, fn, reads=(), writes=()):
        return self.P.add(eng, fn, reads, writes)

    def I(self, eng, name, reads, writes, **kw):
        return self.P.add(eng, lambda e, name=name, kw=kw: getattr(e, name)(**kw), reads, writes)

    def MM(self, reads, writes, mms):
        def fn(e, mms=mms):
            ins = None
            for (o, l, r, st, sp) in mms:
                ins = e.matmul(o, l, r, start=st, stop=sp)
            return ins
        return self.P.add("pe", fn, reads, writes)

    def dma(self, eng, out, in_, reads, writes, key=None):
        if key is None:
            key = writes[0]
        return self.P.add(eng, lambda e: e.dma_start(out=out, in_=in_), reads, writes, dma=key)

    def wload(self, src_ap, kc, width=128):
        s = self.wslot
        self.wslot = (self.wslot + 1) % self.NW
        dst = self.wring[:, s, 0:kc * width].rearrange("p (k n) -> p k n", k=kc)
        self.dma("pool", dst, src_ap, reads=[], writes=[("w", s)])
        return s, dst

    def setup(self):
        nc = self.nc
        self.hin = self.din("hin", [128, KC, TT])
        self.pc = self.din("pc", [128, 32])
        self.dshapes = {
            "w1r": [4, 64, 128, KC, 128], "w2r": [4, 4, 16, 128, KC, 128], "winr": [2, 40, 128, KC, 128],
            "woutr": [2, 16, 128, RC, 128], "gwr": [2, 2, 4, 128, 13, 128], "wkr": [16, 128, KC, 128],
            "wvr": [4, 128, KC, 512], "wqr": [2, 16, 128, KC, 128], "wor": [2, 16, 128, KC, 128],
            "rgv": [2, 128, 8, RC], "blam": [128, 2, 4], "sg": [128, 2, 2], "rot": [128, 128], "band": [128, 4, 512],
            "cosd": [128, TT], "sind": [128, TT], "ident": [128, 128], "wor2": [2, 8, 2, 128, 2048],
        }
        self.dcache = {}
        self.gains_d = self.din("gains", [128, 10, KC])
        if self.last:
            self.outd = self.dout("out", [128, KC, OWN])
        else:
            self.houtd = self.dout("hout", [128, KC, TT])
        self.ab_d = nc.dram_tensor("ab_scr", [2, RC, 128, TT], F32)
        self.cc_in = nc.dram_tensor("cc_in", [128, 40], F32)
        self.cc_out = nc.dram_tensor("cc_out", [4 * 128, 40], F32)
        self.hh_in = nc.dram_tensor("hh_in", [128, 48], F32)
        self.hh_out = nc.dram_tensor("hh_out", [4 * 128, 48], F32)
        self.kt_loc = [nc.dram_tensor("kt_loc%d" % h, [256, OWN], BF16) for h in range(8)]
        self.kt_all = [nc.dram_tensor("kt_all%d" % h, [4 * 256, OWN], BF16) for h in range(8)]
        self.v_loc = [nc.dram_tensor("v_loc%d" % h, [OWN, 256], BF16) for h in range(8)]
        self.v_all = [nc.dram_tensor("v_all%d" % h, [4 * OWN, 256], BF16) for h in range(8)]
        self.vm_d = nc.dram_tensor("vm_d", [NMETA, D], BF16)
        self.q_scr = [nc.dram_tensor("q_scr%d" % i, [128, OWN], BF16) for i in range(16)]
        if "kvin" in self.stages:
            self.kt_all_in = self.din("kt_all_in", [4 * 16 * 128, OWN], BF16)
            self.v_all_in = self.din("v_all_in", [4 * 8 * 128, D], BF16)
            self.ktm_in = self.din("ktm_in", [128, 16, NMETA], BF16)
            self.vm_in = self.din("vm_in", [NMETA, D], BF16)
        if "kvout" in self.stages:
            self.kt_all_o = self.dout("kt_all_o", [4 * 16 * 128, OWN], BF16)
            self.v_all_o = self.dout("v_all_o", [4 * 8 * 128, D], BF16)
            self.ktm_o = self.dout("ktm_o", [128, 16, NMETA], BF16)
            self.vm_o = self.dout("vm_o", [NMETA, D], BF16)

        self.hT = self.sb("hT", [128, KC, TT], F32)
        self.uT = self.sb("uT", [128, KC, TT], BF16)
        self.NW = 4
        self.wring = self.sb("wring", [128, self.NW, RC * 128], BF16)
        self.gains = self.sb("gains", [128, 10, KC], F32)
        self.pcs = self.sb("pcs", [128, 32], F32)
        self.ones_bf = self.sb("ones_bf", [128, 128], BF16)
        self.ones1 = self.sb("ones1", [128, 128], BF16)
        self.ones256 = self.sb("ones256", [128, 128], BF16)
        self.ones_f = self.sb("ones_f", [128, 128], F32)
        self.rstd = self.sb("rstd", [128, TT], F32)
        self.epsc = self.sb("epsc", [128, 2], F32)
        self.sqr = self.sb("sqr", [128, 4, 512], BF16)
        self.ktm = self.sb("ktm", [128, 16, NMETA], BF16)
        ARENA = 16700
        self.arena = self.sb("arena", [128, ARENA], F32)
        self.fence = self.sb("fence", [128, 2], F32)
        self.rgs = self.sb("rgs", [128, 8, RC], F32)
        self.cneg = self.sb("cneg", [128, RC], F32)
        self.ST = self.sb("ST", [128, 40], F32)
        self.SM = self.sb("SM", [128, RC], F32)
        self.RS = self.sb("RS", [128, RC, 2], F32)
        self.HIN = self.sb("HIN", [128, RC], F32)
        self.G4 = self.sb("G4", [128, 4, 40], F32)
        self.tmp20 = self.sb("tmp20", [128, RC], F32)
        self.HG = self.sb("HG", [128, 4, 48], F32)
        self.gw = self.sb("gw", [128, 2, 13 * 128], BF16)
        self.negpi = self.sb("negpi", [128, 1], F32)
        self.rot_sb = self.sb("rot_sb", [128, 128], F32)
        self.band_bf = self.sb("band_bf", [128, 4, 512], BF16)
        self.lamt = self.sb("lamt", [128, 8], F32)
        self.sgc = self.sb("sgc", [128, 2], F32)
        self.invf_sb = self.sb("invf_sb", [128, 1], F32)
        self.psb = [self.es.enter_context(nc.psum_tensor("ps%d" % i, [128, 512], F32)) for i in range(8)]

        P = self.P
        self.dma("sp", self.gains[:, :, :], self.gains_d[:, :, :], [], ["gains"])
        self.dma("sp", self.pcs[:, :], self.pc[:, :], [], ["pcs"])
        for c in range(KC):
            self.dma("sp", self.hT[:, c, :], self.hin[:, c, :], [], [("hT", c, 0), ("hT", c, 1), ("hT", c, 2)], key="hTld")
        self.op("dve", lambda e: e.memset(self.fence[:, :], 0.0), [], [("hT", c, t) for c in range(KC) for t in range(3)])
        self.op("dve", lambda e: e.memset(self.ones_bf[:, :], 1.0 / 2048.0), [], ["ones_bf"])
        self.op("dve", lambda e: e.memset(self.ones1[:, :], 1.0), [], ["ones1"])
        self.op("dve", lambda e: e.memset(self.ones256[:, :], 1.0 / 256.0), [], ["ones256"])
        self.op("dve", lambda e: e.memset(self.ones_f[:, :], 1.0), [], ["ones_f"])
        self.op("dve", lambda e: e.memset(self.epsc[:, 0:1], 1e-6), [], ["epsc"])
        self.op("dve", lambda e: e.memset(self.epsc[:, 1:2], 1e-5), ["epsc"], ["epsc"])
        self.op("dve", lambda e: e.memset(self.uT[:, :, :], 0.0), [], [("uT", c, t) for c in range(KC) for t in range(3)])
        self.op("dve", lambda e: e.memset(self.arena[:, :], 0.0), [], ["AR"])

    def fence_all(self):
        P = self.P
        last = {}
        for i, op in enumerate(P.ops):
            if op["fn"] is None:
                continue
            if op["dma"] is None:
                last[op["eng"]] = i
            else:
                last[("d", op["dma"])] = i
        idx = P.add("dve", lambda e: e.memset(self.fence[:, :], 0.0), [], ["FENCE"])
        P.ops[idx]["deps"].update(last.values())
        for eng in ("pe", "act", "pool", "sp"):
            P.add(eng, None, ["FENCE"], [])

    def coll(self, src, dst, reads, writes, key):
        def fn(e):
            return e.collective_compute("AllGather", ALU.bypass, replica_groups=[[0, 1, 2, 3], [4, 5, 6, 7]],
                                        ins=[src.ap().opt()], outs=[dst.ap().opt()])
        return self.P.add("pool", fn, reads, writes, dma=key, inc=1)

    def halo_exchange(self):
        hk = [("hT", c, 2) for c in range(KC)]
        self.dma("sp", self.hh_in.ap().rearrange("p (c t) -> p c t", c=KC), self.hT[:, :, TT - 3:TT], hk, ["hh_in"])
        self.coll(self.hh_in, self.hh_out, ["hh_in"], ["hh_out"], "cc_h")
        self.dma("sp", self.HG[:, :, :], self.hh_out.ap().rearrange("(r p) c -> p r c", r=4), ["hh_out"], ["HG"])
        h0 = [("hT", c, 0) for c in range(KC)]
        self.I("dve", "tensor_scalar", h0 + ["pcs"], h0, out=self.hT[:, :, 16:19], in0=self.hT[:, :, 13:16],
               scalar1=self.pcs[:, 20:21], scalar2=None, op0=ALU.mult)
        for r in range(4):
            self.I("dve", "scalar_tensor_tensor", h0 + ["pcs", "HG"], h0, out=self.hT[:, :, 16:19],
                   in0=self.HG[:, r, :].rearrange("p (c t) -> p c t", c=KC), scalar=self.pcs[:, 4 + r:5 + r],
                   in1=self.hT[:, :, 16:19], op0=ALU.mult, op1=ALU.add)

    def rglru(self, j):
        ar = self.arena
        one = self.ones_f[:, 0:1]
        self.rmsnorm(j)
        self.fence_all()
        self.dma("sp", self.rgs[:, :, :], self.D("rgv")[j], [], ["rgs"])
        self.I("act", "activation", ["rgs"], ["cneg"], out=self.cneg[:, :], in_=self.rgs[:, 7, :], func=AF.Exp, scale=-1.0)
        self.I("act", "activation", ["cneg", "ones_f"], ["cneg"], out=self.cneg[:, :], in_=self.cneg[:, :], func=AF.Ln, bias=one)
        self.I("act", "mul", ["cneg"], ["cneg"], out=self.cneg[:, :], in_=self.cneg[:, :], mul=-8.0)
        X = ar[:, 0:2092].rearrange("p (s t) -> p s t", s=2)
        C = ar[:, 2092:7307].rearrange("p (c t) -> p c t", c=5)
        Cb = ar[:, 7307:9915].bitcast(BF16)[:, 0:5215].rearrange("p (c t) -> p c t", c=5)
        R1 = ar[:, 9915:12001].rearrange("p (s t) -> p s t", s=2)
        I1 = ar[:, 12001:14087].rearrange("p (s t) -> p s t", s=2)
        T1 = ar[:, 14087:16173].rearrange("p (s t) -> p s t", s=2)
        self.I("dve", "memset", [], [("X", 0), ("X", 1)], ap=X[:, :, 0:3], constant=0.0)
        xs = 0
        rs = 0
        for pr in range(4):
            for g in range(2):
                self.dma("pool", self.gw[:, g, :].rearrange("p (t n) -> p t n", t=13), self.D("gwr")[j, g, pr], [], [("gw", g)])
            for ch in range(5):
                gc = pr * 5 + ch
                sl = xs
                xs ^= 1

                def evx(ti, c0, c1, ps, bk, sl=sl):
                    self.I("act", "activation", [("ps", bk)], [("X", sl)], out=X[:, sl, 3 + c0:3 + c1], in_=ps[:, 0:c1 - c0], func=AF.Copy)
                self.proj(self.D("winr")[j, 20 + gc], KC, self.uT, "uT", (0, 1, 2), evx)
                ck = [("C", ch)]
                self.I("dve", "tensor_scalar", [("X", sl), "rgs"], ck, out=C[:, ch, :], in0=X[:, sl, 0:TT],
                       scalar1=self.rgs[:, 0, gc:gc + 1], scalar2=self.rgs[:, 4, gc:gc + 1], op0=ALU.mult, op1=ALU.add)
                for tap in range(1, 4):
                    self.I("dve", "scalar_tensor_tensor", [("X", sl), "rgs"] + ck, ck, out=C[:, ch, :], in0=X[:, sl, tap:tap + TT],
                           scalar=self.rgs[:, tap, gc:gc + 1], in1=C[:, ch, :], op0=ALU.mult, op1=ALU.add)
                self.I("act", "activation", ck, [("Cb", ch)], out=Cb[:, ch, :], in_=C[:, ch, :], func=AF.Copy)
            for jc in range(5):
                gc = pr * 5 + jc
                sl = rs
                rs ^= 1
                ins = [ic for ic in range(5) if (ic, jc) in GT_IDX]
                for g, (dst, dk) in enumerate(((R1, "R1"), (I1, "I1"))):
                    banks = [self.bank() for _ in range(3)]
                    mms = []
                    for n, ic in enumerate(ins):
                        for ti, (c0, c1) in enumerate(TILES):
                            mms.append((self.psb[banks[ti]][:, 0:c1 - c0], self.gw[:, g, GT_IDX[(ic, jc)] * 128:(GT_IDX[(ic, jc)] + 1) * 128],
                                        Cb[:, ic, c0:c1], n == 0, n == len(ins) - 1))
                    self.MM([("gw", g)] + [("Cb", ic) for ic in ins], [("ps", b) for b in banks], mms)
                    for ti, (c0, c1) in enumerate(TILES):
                        kw = {}
                        wr = [(dk, sl)]
                        if g == 0 and ti > 0:
                            kw["accum_out"] = self.RS[:, gc, ti - 1:ti]
                            wr = wr + [("RS", gc)]
                        self.I("act", "activation", [("ps", banks[ti]), "rgs"], wr, out=dst[:, sl, c0:c1], in_=self.psb[banks[ti]][:, 0:c1 - c0],
                               func=AF.Sigmoid, bias=self.rgs[:, 5 + g, gc:gc + 1], **kw)
                self.I("act", "activation", [("R1", sl), "cneg"], [("R1", sl)], out=R1[:, sl, :], in_=R1[:, sl, :], func=AF.Exp, scale=self.cneg[:, gc:gc + 1])
                self.I("dve", "tensor_tensor", [("R1", sl)], [("T1", sl)], out=T1[:, sl, :], in0=R1[:, sl, :], in1=R1[:, sl, :], op=ALU.mult)
                self.I("act", "activation", [("T1", sl), "ones_f"], [("T1", sl)], out=T1[:, sl, :], in_=T1[:, sl, :], func=AF.Sqrt, scale=-1.0, bias=one)
                self.I("dve", "tensor_tensor", [("I1", sl), ("C", jc)], [("I1", sl)], out=I1[:, sl, :], in0=I1[:, sl, :], in1=C[:, jc, :], op=ALU.mult)
                self.I("dve", "tensor_tensor", [("I1", sl), ("T1", sl)], [("I1", sl)], out=I1[:, sl, :], in0=I1[:, sl, :], in1=T1[:, sl, :], op=ALU.mult)
                self.I("dve", "tensor_tensor_scan", [("R1", sl), ("I1", sl)], [("T1", sl)], out=T1[:, sl, 0:16], data0=R1[:, sl, 0:16], data1=I1[:, sl, 0:16],
                       initial=0.0, op0=ALU.mult, op1=ALU.add)
                self.I("dve", "tensor_tensor_scan", [("R1", sl), ("I1", sl)], [("T1", sl)], out=T1[:, sl, C_OWN:TT], data0=R1[:, sl, C_OWN:TT], data1=I1[:, sl, C_OWN:TT],
                       initial=0.0, op0=ALU.mult, op1=ALU.add)
                self.I("dve", "tensor_copy", [("T1", sl)], [("SM", gc)], out=self.SM[:, gc:gc + 1], in_=T1[:, sl, 15:16])
                self.I("dve", "tensor_copy", [("T1", sl)], [("ST", gc)], out=self.ST[:, 20 + gc:21 + gc], in_=T1[:, sl, TT - 1:TT])
                self.dma("sp", self.ab_d[0, gc], R1[:, sl, :], [("R1", sl)], [("abd", 0, gc)], key=("abst", 0, sl))
                self.dma("sp", self.ab_d[1, gc], I1[:, sl, :], [("I1", sl)], [("abd", 1, gc)], key=("abst", 1, sl))
        allrs = [("RS", c) for c in range(RC)]
        allst = [("ST", c) for c in range(RC)]
        self.I("dve", "tensor_tensor", allrs, ["tmp20"], out=self.tmp20[:, :], in0=self.RS[:, :, 0], in1=self.RS[:, :, 1], op=ALU.add)
        self.I("dve", "tensor_tensor", ["tmp20", "cneg"], ["tmp20"], out=self.tmp20[:, :], in0=self.tmp20[:, :], in1=self.cneg[:, :], op=ALU.mult)
        self.I("act", "activation", ["tmp20"] + allst, allst, out=self.ST[:, 0:20], in_=self.tmp20[:, :], func=AF.Exp)
        self.dma("sp", self.cc_in.ap(), self.ST[:, :], allst, ["cc_in"])
        self.coll(self.cc_in, self.cc_out, ["cc_in"], ["cc_out"], "cc_c")
        self.dma("sp", self.G4[:, :, :], self.cc_out.ap().rearrange("(r p) c -> p r c", r=4), ["cc_out"], ["G4"])
        allsm = [("SM", c) for c in range(RC)]
        self.I("dve", "tensor_copy", allsm, ["HIN"], out=self.HIN[:, :], in_=self.SM[:, :])
        for r in range(3):
            self.I("dve", "tensor_tensor", ["G4", "HIN"], ["tmp20"], out=self.tmp20[:, :], in0=self.G4[:, r, 0:20], in1=self.HIN[:, :], op=ALU.mult)
            self.I("dve", "tensor_tensor", ["G4", "tmp20"], ["tmp20"], out=self.tmp20[:, :], in0=self.tmp20[:, :], in1=self.G4[:, r, 20:40], op=ALU.add)
            self.I("dve", "tensor_tensor", ["HIN", "tmp20"], ["tmp20"], out=self.tmp20[:, :], in0=self.tmp20[:, :], in1=self.HIN[:, :], op=ALU.subtract)
            self.I("dve", "scalar_tensor_tensor", ["HIN", "tmp20", "pcs"], ["HIN"], out=self.HIN[:, :], in0=self.tmp20[:, :], scalar=self.pcs[:, r:r + 1],
                   in1=self.HIN[:, :], op0=ALU.mult, op1=ALU.add)
        self.fence_all()
        A2r = ar[:, 0:2086].rearrange("p (s t) -> p s t", s=2)
        B2r = ar[:, 2086:4172].rearrange("p (s t) -> p s t", s=2)
        HS = ar[:, 4172:5215].rearrange("p (s t) -> p s t", s=1)
        GG = ar[:, 5215:6258].rearrange("p (s t) -> p s t", s=1)
        Y = ar[:, 6258:16688].bitcast(BF16).rearrange("p (c t) -> p c t", c=RC)
        self.I("dve", "memset", [], [("HS", 0)], ap=HS[:, :, :], constant=0.0)
        for gc in range(RC):
            sl = 0
            ab = gc % 2
            A2 = A2r[:, ab, :]
            B2 = B2r[:, ab, :]
            self.dma("sp", A2, self.ab_d[0, gc], [("abd", 0, gc)], [("A2", ab)])
            self.dma("sp", B2, self.ab_d[1, gc], [("abd", 1, gc)], [("B2", ab)])
            self.I("dve", "tensor_tensor_scan", [("A2", ab), ("B2", ab)], [("HS", sl)], out=HS[:, sl, 0:16], data0=A2[:, 0:16], data1=B2[:, 0:16],
                   initial=0.0, op0=ALU.mult, op1=ALU.add)
            self.I("dve", "tensor_tensor_scan", [("A2", ab), ("B2", ab), "HIN"], [("HS", sl)], out=HS[:, sl, C_OWN:TT], data0=A2[:, C_OWN:TT], data1=B2[:, C_OWN:TT],
                   initial=self.HIN[:, gc:gc + 1], op0=ALU.mult, op1=ALU.add)

            def evg(ti, c0, c1, ps, bk, sl=sl, gc=gc):
                self.I("act", "activation", [("ps", bk)], [("GG", sl, ti)], out=GG[:, sl, c0:c1], in_=ps[:, 0:c1 - c0], func=GELU)
                self.I("dve", "tensor_tensor", [("GG", sl, ti), ("HS", sl)], [("Y", gc, ti)], out=Y[:, gc, c0:c1], in0=GG[:, sl, c0:c1], in1=HS[:, sl, c0:c1], op=ALU.mult)
            self.proj(self.D("winr")[j, gc], KC, self.uT, "uT", (0, 1, 2), evg)
        for oc in range(KC):
            def evo(ti, c0, c1, ps, bk, oc=oc):
                self.I("dve", "tensor_tensor", [("ps", bk), ("hT", oc, ti)], [("hT", oc, ti)],
                       out=self.hT[:, oc, c0:c1], in0=ps[:, 0:c1 - c0], in1=self.hT[:, oc, c0:c1], op=ALU.add)
            self.proj(self.D("woutr")[j, oc], RC, Y, "Y", (0, 1, 2), evo)
        self.fence_all()

    def rsqrt_ps(self, ps, bk, w, c0, c1, ti, eps):
        ec = self.epsc[:, 0:1] if eps == 1e-6 else self.epsc[:, 1:2]
        self.I("act", "activation", [("ps", bk), "epsc"], [("rstd", ti)], out=self.rstd[:, c0:c1], in_=ps[:, 0:w], func=AF.Ln, bias=ec)
        self.I("act", "activation", [("rstd", ti)], [("rstd", ti)], out=self.rstd[:, c0:c1], in_=self.rstd[:, c0:c1], func=AF.Exp, scale=-0.5)

    def sumsq_rstd(self, ti, eps=1e-6):
        c0, c1 = TILES[ti]
        w = c1 - c0
        bk = self.bank()
        ps = self.psb[bk]
        for c in range(KC):
            sl = self.sqr_i
            self.sqr_i = (self.sqr_i + 1) % 4
            self.I("act", "activation", [("hT", c, ti)], [("sqr", sl)], out=self.sqr[:, sl, 0:w], in_=self.hT[:, c, c0:c1], func=AF.Square)
            self.MM([("sqr", sl), "ones_bf"], [("ps", bk)], [(ps[:, 0:w], self.ones_bf[:, :], self.sqr[:, sl, 0:w], c == 0, c == KC - 1)])
        self.rsqrt_ps(ps, bk, w, c0, c1, ti, eps)

    def rmsnorm(self, gi, tiles=(0, 1, 2)):
        for ti in tiles:
            c0, c1 = TILES[ti]
            self.sumsq_rstd(ti)
            for c in range(KC):
                self.I("dve", "scalar_tensor_tensor", [("hT", c, ti), ("rstd", ti), "gains"], [("uT", c, ti)],
                       out=self.uT[:, c, c0:c1], in0=self.hT[:, c, c0:c1], scalar=self.gains[:, gi, c:c + 1],
                       in1=self.rstd[:, c0:c1], op0=ALU.mult, op1=ALU.mult)

    def proj(self, wsrc, nkc, src, srckey, tiles, evac):
        s, wt = self.wload(wsrc, nkc)
        banks = {ti: self.bank() for ti in tiles}
        mms = []
        for k in range(nkc):
            for ti in tiles:
                c0, c1 = TILES[ti]
                mms.append((self.psb[banks[ti]][:, 0:c1 - c0], wt[:, k, :], src[:, k, c0:c1], k == 0, k == nkc - 1))
        self.MM([("w", s)] + [(srckey, k, ti) for k in range(nkc) for ti in tiles], [("ps", banks[ti]) for ti in tiles], mms)
        for ti in tiles:
            c0, c1 = TILES[ti]
            evac(ti, c0, c1, self.psb[banks[ti]], banks[ti])

    CS0 = 14614

    def rope_tables(self):
        ar = self.arena
        self.cosT = ar[:, self.CS0:self.CS0 + TT]
        self.sinT = ar[:, self.CS0 + TT:self.CS0 + 2 * TT]
        self.Lown = ar[:, 14300:14556].bitcast(BF16).rearrange("p (r n) -> p r n", r=4)
        self.dma("sp", self.cosT, self.D("cosd")[:, :], [], ["cosT"])
        self.dma("sp", self.sinT, self.D("sind")[:, :], [], ["sinT"])
        self.dma("sp", self.rot_sb[:, :], self.D("rot")[:, :], [], ["rot"])
        self.dma("pool", self.band_bf[:, :, :], self.D("band")[:, :, :], [], ["band"])
        self.dma("sp", self.ones_f[:, :], self.D("ident")[:, :], ["ones_f"], ["ones_f"])
        self.I("dve", "tensor_copy", ["ones_f"], [("Lown", 0)], out=self.Lown[:, 0, :], in_=self.ones_f[:, :])
        self.I("dve", "memset", [("Lown", 0)], ["ones_f"], ap=self.ones_f[:, :], constant=1.0)

    def rope_evac(self, ps, bk, c0, c1, xq, xk, t1, t1k, t2, t2k, dst, dkeys):
        w = c1 - c0
        self.I("act", "activation", [("ps", bk)], [xk], out=xq[:, 0:w], in_=ps[:, 0:w], func=AF.Copy)
        b2 = self.bank()
        self.MM([xk, "rot"], [("ps", b2)], [(self.psb[b2][:, 0:w], self.rot_sb[:, :], xq[:, 0:w], True, True)])
        self.I("dve", "tensor_tensor", [xk, "cosT"], [t1k], out=t1[:, 0:w], in0=xq[:, 0:w], in1=self.cosT[:, c0:c1], op=ALU.mult)
        self.I("dve", "tensor_tensor", [("ps", b2), "sinT"], [t2k], out=t2[:, 0:w], in0=self.psb[b2][:, 0:w], in1=self.sinT[:, c0:c1], op=ALU.mult)
        self.I("dve", "tensor_tensor", [t1k, t2k], dkeys, out=dst, in0=t1[:, 0:w], in1=t2[:, 0:w], op=ALU.add)

    def kv(self):
        ar = self.arena
        self.fence_all()
        self.rope_tables()
        self.rmsnorm(6)
        F = ar[:, 2 * TT:2 * TT + 6 * 512].rearrange("p (s t) -> p s t", s=6)
        o = 2 * TT + 6 * 512
        kto = ar[:, o:o + 1024].bitcast(BF16).rearrange("p (s t) -> p s t", s=2)
        o += 1024
        vo = ar[:, o:o + 512].bitcast(BF16).rearrange("p (s t) -> p s t", s=2)
        vs = 0
        for (chunks) in ([0, 1, 2, 3, 8], [4, 5, 6, 7]):
            for cb in range(4):
                banks = {tc: self.bank() for tc in chunks}
                for kq in range(4):
                    s_, wt = self.wload(self.D("wvr")[cb, :, kq * 4:(kq + 1) * 4, :], 4, 512)
                    mms = []
                    for k4 in range(4):
                        kc = kq * 4 + k4
                        for tc in chunks:
                            if tc == 8:
                                lcols = (0, 16)
                            else:
                                lcols = (C_OWN + tc * 128, C_OWN + (tc + 1) * 128)
                            m = lcols[1] - lcols[0]
                            mms.append((self.psb[banks[tc]][0:m, :], self.uT[:, kc, lcols[0]:lcols[1]], wt[:, k4, :], kc == 0, kc == KC - 1))
                    self.MM([("w", s_)] + [("uT", kc, t) for kc in range(kq * 4, kq * 4 + 4) for t in range(3)], [("ps", banks[tc]) for tc in chunks], mms)
                for tc in chunks:
                    sl = vs
                    vs ^= 1
                    m = 16 if tc == 8 else 128
                    self.I("act", "activation", [("ps", banks[tc])], [("vo", sl)], out=vo[0:m, sl, :], in_=self.psb[banks[tc]][0:m, :], func=AF.Copy)
                    if tc == 8:
                        self.dma("sp", self.vm_d.ap()[:, cb * 512:(cb + 1) * 512], vo[0:16, sl, :], [("vo", sl)], [("vmd", cb)], key=("vst", sl))
                    else:
                        for hh in range(2):
                            self.dma("sp", self.v_loc[cb * 2 + hh].ap()[tc * 128:(tc + 1) * 128, :], vo[:, sl, hh * 256:(hh + 1) * 256], [("vo", sl)],
                                     [("vl", tc, cb, hh)], key=("vst", sl, hh))
        for h in range(8):
            self.coll(self.v_loc[h], self.v_all[h], [("vl", tc, h // 2, h % 2) for tc in range(8)], [("v_all", h)], ("cc_v", h))
        fi = 0
        for oc in range(16):
            ks = oc % 2

            def evk(ti, c0, c1, ps, bk, oc=oc, ks=ks):
                nonlocal fi
                a, b, c = fi % 6, (fi + 1) % 6, (fi + 2) % 6
                fi += 3
                if ti == 0:
                    dst = self.ktm[:, oc, :]
                    dk = [("ktm", oc)]
                    cc0, cc1 = 0, 16
                else:
                    dst = kto[:, ks, c0 - C_OWN:c1 - C_OWN]
                    dk = [("kto", ks, ti)]
                    cc0, cc1 = c0, c1
                self.rope_evac(ps, bk, cc0, cc1, F[:, a, :], ("F", a), F[:, b, :], ("F", b), F[:, c, :], ("F", c), dst, dk)
            self.proj(self.D("wkr")[oc], KC, self.uT, "uT", (0, 1, 2), evk)
            self.dma("sp", self.kt_loc[oc // 2].ap()[(oc % 2) * 128:(oc % 2 + 1) * 128, :], kto[:, ks, :], [("kto", ks, 1), ("kto", ks, 2)], [("ktl", oc)], key=("kst", ks))
            if oc % 2 == 1:
                self.coll(self.kt_loc[oc // 2], self.kt_all[oc // 2], [("ktl", oc - 1), ("ktl", oc)], [("kt_all", oc // 2)], ("cc_k", oc // 2))
        self.fence_all()

    def attn(self, j):
        layer = 2 + j
        lam_init = 0.8 - 0.6 * math.exp(-0.3 * layer)
        ar = self.arena
        self.fence_all()
        self.rmsnorm(7 + j, tiles=(1, 2))
        KTh = ar[:, 0:4096].bitcast(BF16).rearrange("p (c r t) -> p c r t", c=2, r=4)
        Vh = ar[:, 4096:8192 + 128].bitcast(BF16).rearrange("p (k e) -> p k e", e=256)
        o = 8192 + 128
        QTh = ar[:, o:o + 1024].bitcast(BF16).rearrange("p (c t) -> p c t", c=2)
        o += 1024
        OTh = ar[:, o:o + 1024].bitcast(BF16).rearrange("p (c t) -> p c t", c=2)
        o += 1024
        Pt = ar[:, o:o + 768].bitcast(BF16).rearrange("p (s t) -> p s t", s=3)
        o += 768
        F = ar[:, o:o + 3072].rearrange("p (s t) -> p s t", s=6)
        o += 3072
        sqb = self.sqr
        assert o <= 14300, o
        self.dma("sp", self.lamt[:, 0:4], self.D("blam")[:, j, :], [], ["lamt"])
        self.dma("sp", self.sgc[:, :], self.D("sg")[:, j, :], [], ["sgc"])
        self.I("dve", "tensor_tensor", ["lamt"], ["lamt2"], out=self.lamt[:, 4:5], in0=self.lamt[:, 0:1], in1=self.lamt[:, 1:2], op=ALU.mult)
        self.I("dve", "tensor_tensor", ["lamt", "lamt2"], ["lamt2"], out=self.lamt[:, 5:6], in0=self.lamt[:, 2:3], in1=self.lamt[:, 3:4], op=ALU.mult)
        bl = self.bank()
        self.MM(["lamt2", "ones_f"], [("ps", bl)], [(self.psb[bl][:, 0:2], self.ones_f[:, :], self.lamt[:, 4:6], True, True)])
        self.I("act", "activation", [("ps", bl)], ["lamt3"], out=self.lamt[:, 6:8], in_=self.psb[bl][:, 0:2], func=AF.Exp)
        self.I("dve", "tensor_tensor", ["lamt3"], ["neglam"], out=self.lamt[:, 4:5], in0=self.lamt[:, 7:8], in1=self.lamt[:, 6:7], op=ALU.subtract)
        self.I("dve", "tensor_scalar", ["neglam"], ["neglam"], out=self.lamt[:, 4:5], in0=self.lamt[:, 4:5], scalar1=-lam_init, scalar2=None, op0=ALU.add)
        self.I("dve", "tensor_scalar", ["sgc"], ["sgc"], out=self.sgc[:, :], in0=self.sgc[:, :], scalar1=1.0 - lam_init, scalar2=None, op0=ALU.mult)
        neglam = self.lamt[:, 4:5]
        pi = 0
        fi = 0
        for oc in range(16):
            qs = oc % 2

            def evq(ti, c0, c1, ps, bk, qs=qs):
                nonlocal fi
                a, b, c = fi % 6, (fi + 1) % 6, (fi + 2) % 6
                fi += 3
                self.rope_evac(ps, bk, c0, c1, F[:, a, :], ("F", a), F[:, b, :], ("F", b), F[:, c, :], ("F", c),
                               QTh[:, qs, c0 - C_OWN:c1 - C_OWN], [("QS", qs, ti)])
            self.proj(self.D("wqr")[j, oc], KC, self.uT, "uT", (1, 2), evq)
            self.dma("sp", self.q_scr[oc].ap(), QTh[:, qs, :], [("QS", qs, 1), ("QS", qs, 2)], [("qscr", oc)], key=("qst", qs))
        self.fence_all()
        uflat = self.uT[:, :, :].rearrange("p c t -> p (c t)")
        KTb = [KTh, uflat[:, 0:8192].rearrange("p (c r t) -> p c r t", c=2, r=4)]
        Vb = [Vh, uflat[:, 8192:8192 + 33 * 256].rearrange("p (k e) -> p k e", e=256)]

        def load_kv(hd):
            bf = hd % 2
            K_, V_ = KTb[bf], Vb[bf]
            for cp in range(2):
                for r in range(3):
                    row = r * 256 + cp * 128
                    self.dma("sp", K_[:, cp, r, :], self.kt_all[hd].ap()[row:row + 128, :], [("kt_all", hd)], [("KTh", bf, cp, r)])
                self.dma("sp", K_[:, cp, 3, :], self.kt_loc[hd].ap()[cp * 128:(cp + 1) * 128, :], [("ktl", hd * 2 + cp)], [("KTh", bf, cp, 3)])
            for r in range(3):
                self.dma("sp", V_[:, r * 8:(r + 1) * 8, :], self.v_all[hd].ap()[r * 1024:(r + 1) * 1024, :].rearrange("(c p) e -> p c e", p=128),
                         [("v_all", hd)], [("Vh", bf, r)])
            self.dma("sp", V_[:, 24:32, :], self.v_loc[hd].ap().rearrange("(c p) e -> p c e", p=128),
                     [("vl", tc, hd // 2, hd % 2) for tc in range(8)], [("Vh", bf, 3)])
            self.dma("sp", V_[0:16, 32, :], self.vm_d.ap()[:, hd * 256:(hd + 1) * 256], [("vmd", cb) for cb in range(4)], [("Vh", bf, 4)])

        load_kv(0)
        for hd in range(8):
            bf = hd % 2
            KTh, Vh = KTb[bf], Vb[bf]
            for cp in range(2):
                self.dma("sp", QTh[:, cp, :], self.q_scr[hd * 2 + cp].ap(), [("qscr", hd * 2 + cp)], [("QTh", cp, 1), ("QTh", cp, 2)], key=("qld", cp))
            if hd + 1 < 8:
                load_kv(hd + 1)
            for qt in range(2):
                q0 = qt * 512
                accs = {0: (2, 3, 4), 1: (5, 6, 7)}
                items = [("m", 0, 0)] + [("r", r, c) for r in range(3) for c in range(8)] + [("o", 3, c) for c in range(4 * qt + 4)]
                flat = [(cp, n, it) for cp in range(2) for n, it in enumerate(items)]

                def info(cp, kind, r, c):
                    if kind == "m":
                        return (16, self.ktm[:, hd * 2 + cp, :], [("ktm", hd * 2 + cp)], 32, [("Vh", bf, 4)], None, None)
                    lk = KTh[:, cp, r, c * 128:(c + 1) * 128]
                    if kind == "r":
                        bias, band = self.pcs[:, 12 + r:13 + r], None
                    else:
                        bias, band = None, (None if c < 4 * qt else c - 4 * qt)
                    return (128, lk, [("KTh", bf, cp, r)], r * 8 + c, [("Vh", bf, r)], bias, band)

                def emit_S(i):
                    cp, n, (kind, r, c) = flat[i]
                    nk, lk, lkk, vidx, vk, bias, band = info(cp, kind, r, c)
                    sb_ = i % 2
                    if band is None:
                        self.MM(lkk + [("QTh", cp, 1 + qt)], [("ps", sb_)], [(self.psb[sb_][0:nk, :], lk, QTh[:, cp, q0:q0 + 512], True, True)])
                    else:
                        self.MM(lkk + [("QTh", cp, 1 + qt), ("Lown", 0), "band"], [("ps", sb_)],
                                [(self.psb[sb_][0:nk, :], lk, QTh[:, cp, q0:q0 + 512], True, False),
                                 (self.psb[sb_][0:nk, 0:128 * (band + 1)], self.Lown[:, 0, :], self.band_bf[:, band, 0:128 * (band + 1)], False, True)])

                def emit_rest(i):
                    nonlocal pi
                    cp, n, (kind, r, c) = flat[i]
                    nk, lk, lkk, vidx, vk, bias, band = info(cp, kind, r, c)
                    bO0, bO1, bS = accs[cp]
                    first, lastk = n == 0, n == len(items) - 1
                    sb_ = i % 2
                    ps = self.psb[sb_]
                    sl = pi % 3
                    pi += 1
                    kw = {} if bias is None else {"bias": bias[0:nk, :]}
                    self.I("act", "activation", [("ps", sb_), "pcs"], [("Pt", sl)], out=Pt[0:nk, sl, :], in_=ps[0:nk, :], func=AF.Exp, scale=SCALE, **kw)
                    self.MM(vk + [("Pt", sl), "ones1"], [("ps", bO0), ("ps", bO1), ("ps", bS)],
                            [(self.psb[bO0][:, :], Vh[0:nk, vidx, 0:128], Pt[0:nk, sl, :], first, lastk),
                             (self.psb[bO1][:, :], Vh[0:nk, vidx, 128:256], Pt[0:nk, sl, :], first, lastk),
                             (self.psb[bS][:, :], self.ones1[0:nk, :], Pt[0:nk, sl, :], first, lastk)])

                emit_S(0)
                for i in range(len(flat)):
                    if i + 1 < len(flat):
                        emit_S(i + 1)
                    emit_rest(i)
                rc0, rc1, os0, os1, t0, t1 = F[:, 4, :], F[:, 5, :], F[:, 2, :], F[:, 3, :], F[:, 0, :], F[:, 1, :]
                self.I("dve", "reciprocal", [("ps", accs[0][2])], [("F", 4)], out=rc0, in_=self.psb[accs[0][2]][:, :])
                self.I("act", "activation", [("ps", accs[1][0])], [("F", 0)], out=t0, in_=self.psb[accs[1][0]][:, :], func=AF.Copy)
                self.I("dve", "reciprocal", [("ps", accs[1][2])], [("F", 5)], out=rc1, in_=self.psb[accs[1][2]][:, :])
                self.I("act", "activation", [("ps", accs[1][1])], [("F", 1)], out=t1, in_=self.psb[accs[1][1]][:, :], func=AF.Copy)
                self.I("dve", "tensor_tensor", [("ps", accs[0][0]), ("F", 4)], [("F", 2)], out=os0, in0=self.psb[accs[0][0]][:, :], in1=rc0, op=ALU.mult)
                self.I("dve", "tensor_tensor", [("ps", accs[0][1]), ("F", 4)], [("F", 3)], out=os1, in0=self.psb[accs[0][1]][:, :], in1=rc0, op=ALU.mult)
                self.I("dve", "tensor_scalar", [("F", 5), "neglam"], [("F", 5)], out=rc1, in0=rc1, scalar1=neglam, scalar2=None, op0=ALU.mult)
                for ec, (osb, fk, tt, tk) in enumerate(((os0, 2, t0, 0), (os1, 3, t1, 1))):
                    self.I("dve", "tensor_tensor", [("F", tk), ("F", 5)], [("F", tk)], out=tt, in0=tt, in1=rc1, op=ALU.mult)
                    self.I("dve", "tensor_tensor", [("F", fk), ("F", tk)], [("F", fk)], out=osb, in0=osb, in1=tt, op=ALU.add)
                    self.I("act", "activation", [("F", fk)], [("sqr", ec)], out=sqb[:, ec, :], in_=osb, func=AF.Square)
                bn = self.bank()
                self.MM([("sqr", 0), ("sqr", 1), "ones256"], [("ps", bn)], [(self.psb[bn][:, :], self.ones256[:, :], sqb[:, 0, :], True, False),
                                                                              (self.psb[bn][:, :], self.ones256[:, :], sqb[:, 1, :], False, True)])
                self.rsqrt_ps(self.psb[bn], bn, 512, 19, 531, 1, 1e-5)
                for ec, (osb, fk) in enumerate(((os0, 2), (os1, 3))):
                    self.I("dve", "scalar_tensor_tensor", [("F", fk), ("rstd", 1), "sgc"], [("OTh", ec, qt)], out=OTh[:, ec, q0:q0 + 512], in0=osb,
                           scalar=self.sgc[:, ec:ec + 1], in1=self.rstd[:, 19:531], op0=ALU.mult, op1=ALU.mult)
            ws = [self.wload(self.D("wor2")[j, hd, kc2], 1, 2048) for kc2 in range(2)]
            for oc in range(KC):
                banks = [self.bank(), self.bank()]
                mms = []
                for kc2 in range(2):
                    for qt in range(2):
                        mms.append((self.psb[banks[qt]][:, :], ws[kc2][1][:, 0, oc * 128:(oc + 1) * 128], OTh[:, kc2, qt * 512:(qt + 1) * 512], kc2 == 0, kc2 == 1))
                self.MM([("w", ws[0][0]), ("w", ws[1][0])] + [("OTh", e, q) for e in range(2) for q in range(2)], [("ps", b) for b in banks], mms)
                for qt in range(2):
                    c0, c1 = TILES[1 + qt]
                    self.I("dve", "tensor_tensor", [("ps", banks[qt]), ("hT", oc, 1 + qt)], [("hT", oc, 1 + qt)],
                           out=self.hT[:, oc, c0:c1], in0=self.psb[banks[qt]][:, :], in1=self.hT[:, oc, c0:c1], op=ALU.add)
        self.fence_all()


    def mlp(self, layer, tiles=(0, 1, 2)):
        self.rmsnorm(2 + layer, tiles)
        hid = self.arena[:, 0:KC * TT // 2].bitcast(BF16).rearrange("p (c t) -> p c t", c=KC)
        off = KC * TT // 2
        sqt = self.arena[:, off:off + 3 * 512].rearrange("p (s t) -> p s t", s=3)
        for grp in range(4):
            for oc in range(16):
                def evac(ti, c0, c1, ps, bk, oc=oc):
                    w = c1 - c0
                    sl = self.sq_i
                    self.sq_i = (self.sq_i + 1) % 3
                    self.I("act", "activation", [("ps", bk)], [("sqt", sl)], out=sqt[:, sl, 0:w], in_=ps[:, 0:w], func=AF.Square)
                    self.I("dve", "scalar_tensor_tensor", [("ps", bk), ("sqt", sl)], [("hid", oc, ti)],
                           out=hid[:, oc, c0:c1], in0=ps[:, 0:w], scalar=0.0, in1=sqt[:, sl, 0:w], op0=ALU.is_gt, op1=ALU.mult)
                self.proj(self.D("w1r")[layer, grp * 16 + oc], KC, self.uT, "uT", tiles, evac)
            for oc2 in range(16):
                def evac2(ti, c0, c1, ps, bk, oc2=oc2):
                    w = c1 - c0
                    self.I("dve", "tensor_tensor", [("ps", bk), ("hT", oc2, ti)], [("hT", oc2, ti)],
                           out=self.hT[:, oc2, c0:c1], in0=ps[:, 0:w], in1=self.hT[:, oc2, c0:c1], op=ALU.add)
                self.proj(self.D("w2r")[layer, grp, oc2], KC, hid, "hid", tiles, evac2)

    def final(self):
        gi = 9
        for ti in (1, 2):
            c0, c1 = TILES[ti]
            self.sumsq_rstd(ti)
            for c in range(KC):
                self.I("dve", "scalar_tensor_tensor", [("hT", c, ti), ("rstd", ti), "gains"], [("hT", c, ti)],
                       out=self.hT[:, c, c0:c1], in0=self.hT[:, c, c0:c1], scalar=self.gains[:, gi, c:c + 1],
                       in1=self.rstd[:, c0:c1], op0=ALU.mult, op1=ALU.mult)
        for c in range(KC):
            self.dma("sp", self.outd[:, c, :], self.hT[:, c, C_OWN:TT], [("hT", c, 1), ("hT", c, 2)], [("outd", c)], key="outst")

    def store_h(self):
        for c in range(KC):
            self.dma("sp", self.houtd[:, c, :], self.hT[:, c, :], [("hT", c, 0), ("hT", c, 1), ("hT", c, 2)], [("houtd", c)], key="outst")

    def finish(self):
        P = self.P
        outs = [i for i, op in enumerate(P.ops) if op["dma"] == "outst" or op["dma"] == "kvst"]
        idx = P.add("sp", None, [], [])
        P.ops[idx]["deps"].update(outs)

    def build(self):
        self.setup()
        for st in self.stages:
            if st.startswith("mlp"):
                L = int(st[3:])
                self.mlp(L, (0, 1, 2) if L < 2 else (1, 2))
            elif st.startswith("rg"):
                self.rglru(int(st[2:]))
            elif st == "halo":
                self.halo_exchange()
            elif st == "kv":
                self.kv()
            elif st.startswith("at"):
                self.attn(int(st[2:]))
        if self.last:
            self.final()
        else:
            self.store_h()
        self.finish()
        self.P.emit(self.nc, self.es)
        self.es.close()
        return self.nc


def _fm(v, nch):
    return np.ascontiguousarray(v.reshape(nch, 128).T)


def _wtiles(w, kc, noc):
    return np.ascontiguousarray(w.reshape(kc, 128, noc, 128).transpose(2, 1, 0, 3))


def prep_shared(inp):
    f = np.float32
    sh = {}
    w1 = inp["mlp_w1"]
    sh["w1r"] = np.stack([_wtiles(w1[l], KC, 64) for l in range(4)])
    w2 = inp["mlp_w2"]
    sh["w2r"] = np.stack([np.stack([_wtiles(w2[l, g * 2048:(g + 1) * 2048], KC, 16) for g in range(4)]) for l in range(4)])
    sh["winr"] = np.stack([_wtiles(inp["a_w_in"][j], KC, 40) for j in range(2)])
    sh["woutr"] = np.stack([_wtiles(inp["a_w_out"][j], RC, 16) for j in range(2)])
    gw = np.zeros((2, 2, 4, 128, 13, 128), f)
    for j in range(2):
        for g, key in enumerate(("a_w_r", "a_w_i")):
            full = np.zeros((DRNN, DRNN), f)
            for n in range(16):
                full[n * 160:(n + 1) * 160, n * 160:(n + 1) * 160] = inp[key][j, n]
            for pr in range(4):
                for ti, (ic, jc) in enumerate(GT):
                    r0 = pr * 640 + ic * 128
                    c0 = pr * 640 + jc * 128
                    gw[j, g, pr, :, ti, :] = full[r0:r0 + 128, c0:c0 + 128]
    sh["gwr"] = gw
    wkv = inp["w_kv"]
    sh["wkr"] = _wtiles(wkv[:, :2048], KC, 16)
    sh["wvr"] = np.ascontiguousarray(wkv[:, 2048:].reshape(KC, 128, 4, 512).transpose(2, 1, 0, 3))
    sh["wqr"] = np.stack([_wtiles(inp["b_w_q"][j], KC, 16) for j in range(2)])
    sh["wor2"] = np.ascontiguousarray(inp["b_w_o"].reshape(2, 8, 2, 128, 2048))
    gains = np.zeros((128, 10, KC), f)
    gl = [inp["a_norm_g"][0], inp["a_norm_g"][1], inp["mlp_norm_g"][0], inp["mlp_norm_g"][1], inp["mlp_norm_g"][2],
          inp["mlp_norm_g"][3], inp["kv_norm_g"], inp["b_norm_g"][0], inp["b_norm_g"][1], inp["final_norm_g"]]
    for i, g in enumerate(gl):
        gains[:, i, :] = _fm(g, KC)
    sh["gains"] = gains
    rgv = np.zeros((2, 128, 8, RC), f)
    for j in range(2):
        for t in range(4):
            rgv[j, :, t, :] = _fm(inp["a_conv_w"][j, t], RC)
        rgv[j, :, 4, :] = _fm(inp["a_conv_b"][j], RC)
        rgv[j, :, 5, :] = _fm(inp["a_b_r"][j], RC)
        rgv[j, :, 6, :] = _fm(inp["a_b_i"][j], RC)
        rgv[j, :, 7, :] = _fm(inp["a_lambda"][j], RC)
    sh["rgv"] = rgv
    sh["blam"] = np.ascontiguousarray(inp["b_lambda"].transpose(2, 0, 1)).astype(f)
    sh["sg"] = np.ascontiguousarray(inp["b_subln_g"].reshape(2, 2, 128).transpose(2, 0, 1)).astype(f)
    rot = np.zeros((128, 128), f)
    for d in range(64):
        rot[d + 64, d] = -1.0
        rot[d, d + 64] = 1.0
    sh["rot"] = rot
    band = np.zeros((128, 4, 512), f)
    k = np.arange(128)[:, None]
    qy = np.arange(512)[None, :]
    for j in range(4):
        band[:, j, :] = np.where(j * 128 + k <= qy, 0.0, NEG)
    sh["band"] = band
    sh["ident"] = np.eye(128, dtype=f)
    return sh


def prep_core(inp, b, q):
    f = np.float32
    x = inp["x"]
    meta = inp["meta_tokens"]
    own = x[b, q * OWN:(q + 1) * OWN]
    halo = meta[13:16] if q == 0 else x[b, q * OWN - 3:q * OWN]
    tok = np.concatenate([meta, halo, own], 0)
    hin = np.ascontiguousarray(tok.T.reshape(KC, 128, TT).transpose(1, 0, 2))
    pc = np.zeros((128, 32), f)
    for r in range(4):
        pc[:, r] = 1.0 if r < q else 0.0
        pc[:, 4 + r] = 1.0 if r == q - 1 else 0.0
        pc[:, 8 + r] = 0.0 if r <= q else NEG
        pc[:, 12 + r] = 0.0 if r < q else NEG
        pc[:, 16 + r] = 0.0 if r == q else 1.0
        pc[:, 24 + r] = 1.0 if r == q else 0.0
    pc[:, 20] = 1.0 if q == 0 else 0.0
    pos = np.zeros((TT,), f)
    pos[0:16] = np.arange(16, dtype=f)
    pos[C_OWN:] = (NMETA + q * OWN + np.arange(OWN)).astype(f)
    inv = (1.0 / (10000.0 ** (np.arange(0, 128, 2, dtype=f) / 128.0))).astype(f)
    ang = (np.tile(inv, 2)[:, None] * pos[None, :]).astype(f)
    return {"hin": hin, "pc": pc, "cosd": np.cos(ang).astype(f), "sind": np.sin(ang).astype(f)}


_CACHE = {}


def run_stages(stages, first, last, core_maps):
    key = (tuple(stages), first, last)
    import time
    t0 = time.time()
    if key not in _CACHE:
        bld = Builder(stages, first, last)
        _CACHE[key] = (bld.build(), bld.declared)
    nc, decl = _CACHE[key]
    core_maps = [{k: v for k, v in m.items() if k in decl} for m in core_maps]
    t1 = time.time()
    r = run_bass_kernel_spmd(nc, core_maps, core_ids=list(range(8))).results
    print("build %.1fs run %.1fs" % (t1 - t0, time.time() - t1), flush=True)
    return r


def kernel(**inputs):
    inp = {k: np.asarray(v) for k, v in inputs.items()}
    sh = prep_shared(inp)
    maps = []
    for b in range(2):
        for q in range(4):
            m = dict(sh)
            m.update(prep_core(inp, b, q))
            maps.append(m)
    res = run_stages(["rg0", "mlp0", "halo", "rg1", "mlp1", "kv", "at0", "mlp2", "at1", "mlp3"], True, True, maps)
    out = np.zeros((2, 4096, D), np.float32)
    for b in range(2):
        for q in range(4):
            o = res[b * 4 + q]["out"]
            out[b, q * OWN:(q + 1) * OWN] = o.transpose(2, 1, 0).reshape(OWN, D)
    return out
```

```python
import math
from contextlib import ExitStack

import numpy as np
import concourse.bass as bass
import concourse.mybir as mybir
from concourse.bass_utils import run_bass_kernel_spmd

F32 = mybir.dt.float32
BF16 = mybir.dt.bfloat16
AF = mybir.ActivationFunctionType
ALU = mybir.AluOpType

D = 2048
KC = 16
NMETA = 16
OWN = 1024
TT = 1043
C_OWN = 19
TILES = [(0, 19), (19, 531), (531, 1043)]
DRNN = 2560
RC = 20
DFF = 8192
NEG = -30000.0
SCALE = 1.0 / math.sqrt(128.0)
GELU = AF.Gelu_apprx_tanh

GT = [(0, 0), (0, 1), (1, 0), (1, 1), (1, 2), (2, 1), (2, 2), (2, 3), (3, 2), (3, 3), (3, 4), (4, 3), (4, 4)]
GT_IDX = {t: i for i, t in enumerate(GT)}


class Prog:
    CE = ("pe", "act", "dve", "pool")

    def __init__(self):
        self.ops = []
        self.lastw = {}
        self.readers = {}
        self.dma_n = {}

    def add(self, eng, fn, reads=(), writes=(), dma=None, inc=16):
        idx = len(self.ops)
        deps = set()
        for r in reads:
            w = self.lastw.get(r)
            if w is not None:
                deps.add(w)
        for k in writes:
            w = self.lastw.get(k)
            if w is not None:
                deps.add(w)
            rd = self.readers.get(k)
            if rd:
                deps.update(rd.values())
        if fn is not None:
            for r in reads:
                self.readers.setdefault(r, {})[(eng if dma is None else ("d", idx))] = idx
            for k in writes:
                self.lastw[k] = idx
                self.readers[k] = {}
        op = {"eng": eng, "fn": fn, "deps": deps, "dma": dma, "inc": inc, "sig": False}
        if dma is not None:
            n = self.dma_n.get(dma, 0) + 1
            self.dma_n[dma] = n
            op["dn"] = n
        self.ops.append(op)
        return idx

    def emit(self, nc, es):
        ops = self.ops
        for op in ops:
            for d in op["deps"]:
                dop = ops[d]
                if dop["dma"] is None:
                    if dop["eng"] == "pe" and op["eng"] == "pe" and op["dma"] is None:
                        continue
                    dop["sig"] = True
        cnt = {e: 0 for e in ("pe", "act", "dve", "pool", "sp")}
        for op in ops:
            if op["dma"] is None and op["sig"]:
                cnt[op["eng"]] += 1
                op["cnt"] = cnt[op["eng"]]
        esem = {e: es.enter_context(nc.semaphore("e_" + e)) for e in ("pe", "act", "dve", "pool", "sp")}
        dsem = {}
        for k in self.dma_n:
            dsem[k] = es.enter_context(nc.semaphore("d_%d" % len(dsem)))
        self.n_sems = len(esem) + len(dsem)
        block = es.enter_context(nc.Block())

        def run(ename, eng):
            waited = {}
            for op in ops:
                if op["eng"] != ename:
                    continue
                need = {}
                for d in op["deps"]:
                    dop = ops[d]
                    if dop["dma"] is not None:
                        s = dsem[dop["dma"]]
                        v = dop["inc"] * dop["dn"]
                    else:
                        if dop["eng"] == "pe" and ename == "pe" and op["dma"] is None:
                            continue
                        s = esem[dop["eng"]]
                        v = dop["cnt"]
                    key = id(s)
                    if v > need.get(key, (None, 0))[1]:
                        need[key] = (s, v)
                for key, (s, v) in need.items():
                    if waited.get(key, 0) >= v:
                        continue
                    eng.wait_ge(s, v)
                    waited[key] = v
                if op["fn"] is None:
                    continue
                ins = op["fn"](eng)
                if op["dma"] is not None:
                    ins.then_inc(dsem[op["dma"]], op["inc"])
                elif op["sig"]:
                    ins.then_inc(esem[ename], 1)

        @block.tensor
        def _(e):
            run("pe", e)

        @block.scalar
        def _(e):
            run("act", e)

        @block.vector
        def _(e):
            run("dve", e)

        @block.gpsimd
        def _(e):
            run("pool", e)

        @block.sync
        def _(e):
            run("sp", e)


class Builder:
    def __init__(self, stages, first, last):
        self.stages = stages
        self.first = first
        self.last = last
        self.P = Prog()
        self.nc = bass.Bass("TRN2", target_bir_lowering=False)
        self.es = ExitStack()
        self.wslot = 0
        self.pb = 0
        self.uid = 0
        self.declared = set()
        self.sq_i = 0
        self.sqr_i = 0

    def din(self, name, shape, dt=F32):
        self.declared.add(name)
        return self.nc.dram_tensor(name, list(shape), dt, kind="ExternalInput").ap()

    def D(self, name):
        if name not in self.dcache:
            self.dcache[name] = self.din(name, self.dshapes[name])
        return self.dcache[name]

    def dout(self, name, shape, dt=F32):
        return self.nc.dram_tensor(name, list(shape), dt, kind="ExternalOutput").ap()

    def sb(self, name, shape, dt):
        return self.es.enter_context(self.nc.sbuf_tensor("s_" + name, list(shape), dt))

    def tkeys(self, name, c, c0, c1):
        ks = []
        for ti, (a, b) in enumerate(TILES):
            if c0 < b and c1 > a:
                ks.append((name, c, ti))
        return ks

    def bank(self):
        b = self.pb
        self.pb = (self.pb + 1) % 8
        return b

    def op(self, eng, fn, reads=(), writes=()):
        return self.P.add(eng, fn, reads, writes)

    def I(self, eng, name, reads, writes, **kw):
        return self.P.add(eng, lambda e, name=name, kw=kw: getattr(e, name)(**kw), reads, writes)

    def MM(self, reads, writes, mms):
        def fn(e, mms=mms):
            ins = None
            for (o, l, r, st, sp) in mms:
                ins = e.matmul(o, l, r, start=st, stop=sp)
            return ins
        return self.P.add("pe", fn, reads, writes)

    def dma(self, eng, out, in_, reads, writes, key=None):
        if key is None:
            key = writes[0]
        return self.P.add(eng, lambda e: e.dma_start(out=out, in_=in_), reads, writes, dma=key)

    def wload(self, src_ap, kc, width=128):
        s = self.wslot
        self.wslot = (self.wslot + 1) % self.NW
        dst = self.wring[:, s, 0:kc * width].rearrange("p (k n) -> p k n", k=kc)
        self.dma("pool", dst, src_ap, reads=[], writes=[("w", s)])
        return s, dst

    def setup(self):
        nc = self.nc
        self.hin = self.din("hin", [128, KC, TT])
        self.pc = self.din("pc", [128, 32])
        self.dshapes = {
            "w1r": [4, 64, 128, KC, 128], "w2r": [4, 4, 16, 128, KC, 128], "winr": [2, 40, 128, KC, 128],
            "woutr": [2, 16, 128, RC, 128], "gwr": [2, 2, 4, 128, 13, 128], "wkr": [16, 128, KC, 128],
            "wvr": [4, 128, KC, 512], "wqr": [2, 16, 128, KC, 128], "wor": [2, 16, 128, KC, 128],
            "rgv": [2, 128, 8, RC], "blam": [128, 2, 4], "sg": [128, 2, 2], "rot": [128, 128], "band": [128, 4, 512],
            "cosd": [128, TT], "sind": [128, TT], "ident": [128, 128], "wor2": [2, 8, 2, 128, 2048],
        }
        self.dcache = {}
        self.gains_d = self.din("gains", [128, 10, KC])
        if self.last:
            self.outd = self.dout("out", [128, KC, OWN])
        else:
            self.houtd = self.dout("hout", [128, KC, TT])
        self.ab_d = nc.dram_tensor("ab_scr", [2, RC, 128, TT], F32)
        self.cc_in = nc.dram_tensor("cc_in", [128, 40], F32)
        self.cc_out = nc.dram_tensor("cc_out", [4 * 128, 40], F32)
        self.hh_in = nc.dram_tensor("hh_in", [128, 48], F32)
        self.hh_out = nc.dram_tensor("hh_out", [4 * 128, 48], F32)
        self.kt_loc = [nc.dram_tensor("kt_loc%d" % h, [256, OWN], BF16) for h in range(8)]
        self.kt_all = [nc.dram_tensor("kt_all%d" % h, [4 * 256, OWN], BF16) for h in range(8)]
        self.v_loc = [nc.dram_tensor("v_loc%d" % h, [OWN, 256], BF16) for h in range(8)]
        self.v_all = [nc.dram_tensor("v_all%d" % h, [4 * OWN, 256], BF16) for h in range(8)]
        self.vm_d = nc.dram_tensor("vm_d", [NMETA, D], BF16)
        if "kvin" in self.stages:
            self.kt_all_in = self.din("kt_all_in", [4 * 16 * 128, OWN], BF16)
            self.v_all_in = self.din("v_all_in", [4 * 8 * 128, D], BF16)
            self.ktm_in = self.din("ktm_in", [128, 16, NMETA], BF16)
            self.vm_in = self.din("vm_in", [NMETA, D], BF16)
        if "kvout" in self.stages:
            self.kt_all_o = self.dout("kt_all_o", [4 * 16 * 128, OWN], BF16)
            self.v_all_o = self.dout("v_all_o", [4 * 8 * 128, D], BF16)
            self.ktm_o = self.dout("ktm_o", [128, 16, NMETA], BF16)
            self.vm_o = self.dout("vm_o", [NMETA, D], BF16)

        self.hT = self.sb("hT", [128, KC, TT], F32)
        self.uT = self.sb("uT", [128, KC, TT], BF16)
        self.NW = 4
        self.wring = self.sb("wring", [128, self.NW, RC * 128], BF16)
        self.gains = self.sb("gains", [128, 10, KC], F32)
        self.pcs = self.sb("pcs", [128, 32], F32)
        self.ones_bf = self.sb("ones_bf", [128, 128], BF16)
        self.ones1 = self.sb("ones1", [128, 128], BF16)
        self.ones256 = self.sb("ones256", [128, 128], BF16)
        self.ones_f = self.sb("ones_f", [128, 128], F32)
        self.rstd = self.sb("rstd", [128, TT], F32)
        self.epsc = self.sb("epsc", [128, 2], F32)
        self.sqr = self.sb("sqr", [128, 4, 512], BF16)
        self.ktm = self.sb("ktm", [128, 16, NMETA], BF16)
        ARENA = 16700
        self.arena = self.sb("arena", [128, ARENA], F32)
        self.fence = self.sb("fence", [128, 2], F32)
        self.rgs = self.sb("rgs", [128, 8, RC], F32)
        self.cneg = self.sb("cneg", [128, RC], F32)
        self.ST = self.sb("ST", [128, 40], F32)
        self.SM = self.sb("SM", [128, RC], F32)
        self.RS = self.sb("RS", [128, RC, 2], F32)
        self.HIN = self.sb("HIN", [128, RC], F32)
        self.G4 = self.sb("G4", [128, 4, 40], F32)
        self.tmp20 = self.sb("tmp20", [128, RC], F32)
        self.HG = self.sb("HG", [128, 4, 48], F32)
        self.gw = self.sb("gw", [128, 2, 13 * 128], BF16)
        self.negpi = self.sb("negpi", [128, 1], F32)
        self.rot_sb = self.sb("rot_sb", [128, 128], F32)
        self.band_bf = self.sb("band_bf", [128, 4, 512], BF16)
        self.lamt = self.sb("lamt", [128, 8], F32)
        self.sgc = self.sb("sgc", [128, 2], F32)
        self.invf_sb = self.sb("invf_sb", [128, 1], F32)
        self.psb = [self.es.enter_context(nc.psum_tensor("ps%d" % i, [128, 512], F32)) for i in range(8)]

        P = self.P
        self.dma("sp", self.gains[:, :, :], self.gains_d[:, :, :], [], ["gains"])
        self.dma("sp", self.pcs[:, :], self.pc[:, :], [], ["pcs"])
        for c in range(KC):
            self.dma("sp", self.hT[:, c, :], self.hin[:, c, :], [], [("hT", c, 0), ("hT", c, 1), ("hT", c, 2)], key="hTld")
        self.op("dve", lambda e: e.memset(self.fence[:, :], 0.0), [], [("hT", c, t) for c in range(KC) for t in range(3)])
        self.op("dve", lambda e: e.memset(self.ones_bf[:, :], 1.0 / 2048.0), [], ["ones_bf"])
        self.op("dve", lambda e: e.memset(self.ones1[:, :], 1.0), [], ["ones1"])
        self.op("dve", lambda e: e.memset(self.ones256[:, :], 1.0 / 256.0), [], ["ones256"])
        self.op("dve", lambda e: e.memset(self.ones_f[:, :], 1.0), [], ["ones_f"])
        self.op("dve", lambda e: e.memset(self.epsc[:, 0:1], 1e-6), [], ["epsc"])
        self.op("dve", lambda e: e.memset(self.epsc[:, 1:2], 1e-5), ["epsc"], ["epsc"])
        self.op("dve", lambda e: e.memset(self.uT[:, :, :], 0.0), [], [("uT", c, t) for c in range(KC) for t in range(3)])
        self.op("dve", lambda e: e.memset(self.arena[:, :], 0.0), [], ["AR"])

    def fence_all(self):
        P = self.P
        last = {}
        for i, op in enumerate(P.ops):
            if op["fn"] is None:
                continue
            if op["dma"] is None:
                last[op["eng"]] = i
            else:
                last[("d", op["dma"])] = i
        idx = P.add("dve", lambda e: e.memset(self.fence[:, :], 0.0), [], ["FENCE"])
        P.ops[idx]["deps"].update(last.values())
        for eng in ("pe", "act", "pool", "sp"):
            P.add(eng, None, ["FENCE"], [])

    def coll(self, src, dst, reads, writes, key):
        def fn(e):
            return e.collective_compute("AllGather", ALU.bypass, replica_groups=[[0, 1, 2, 3], [4, 5, 6, 7]],
                                        ins=[src.ap().opt()], outs=[dst.ap().opt()])
        return self.P.add("pool", fn, reads, writes, dma=key, inc=1)

    def halo_exchange(self):
        hk = [("hT", c, 2) for c in range(KC)]
        self.dma("sp", self.hh_in.ap().rearrange("p (c t) -> p c t", c=KC), self.hT[:, :, TT - 3:TT], hk, ["hh_in"])
        self.coll(self.hh_in, self.hh_out, ["hh_in"], ["hh_out"], "cc_h")
        self.dma("sp", self.HG[:, :, :], self.hh_out.ap().rearrange("(r p) c -> p r c", r=4), ["hh_out"], ["HG"])
        h0 = [("hT", c, 0) for c in range(KC)]
        self.I("dve", "tensor_scalar", h0 + ["pcs"], h0, out=self.hT[:, :, 16:19], in0=self.hT[:, :, 13:16],
               scalar1=self.pcs[:, 20:21], scalar2=None, op0=ALU.mult)
        for r in range(4):
            self.I("dve", "scalar_tensor_tensor", h0 + ["pcs", "HG"], h0, out=self.hT[:, :, 16:19],
                   in0=self.HG[:, r, :].rearrange("p (c t) -> p c t", c=KC), scalar=self.pcs[:, 4 + r:5 + r],
                   in1=self.hT[:, :, 16:19], op0=ALU.mult, op1=ALU.add)

    def rglru(self, j):
        ar = self.arena
        one = self.ones_f[:, 0:1]
        self.rmsnorm(j)
        self.fence_all()
        self.dma("sp", self.rgs[:, :, :], self.D("rgv")[j], [], ["rgs"])
        self.I("act", "activation", ["rgs"], ["cneg"], out=self.cneg[:, :], in_=self.rgs[:, 7, :], func=AF.Exp, scale=-1.0)
        self.I("act", "activation", ["cneg", "ones_f"], ["cneg"], out=self.cneg[:, :], in_=self.cneg[:, :], func=AF.Ln, bias=one)
        self.I("act", "mul", ["cneg"], ["cneg"], out=self.cneg[:, :], in_=self.cneg[:, :], mul=-8.0)
        X = ar[:, 0:2092].rearrange("p (s t) -> p s t", s=2)
        C = ar[:, 2092:7307].rearrange("p (c t) -> p c t", c=5)
        Cb = ar[:, 7307:9915].bitcast(BF16)[:, 0:5215].rearrange("p (c t) -> p c t", c=5)
        R1 = ar[:, 9915:12001].rearrange("p (s t) -> p s t", s=2)
        I1 = ar[:, 12001:14087].rearrange("p (s t) -> p s t", s=2)
        T1 = ar[:, 14087:16173].rearrange("p (s t) -> p s t", s=2)
        self.I("dve", "memset", [], [("X", 0), ("X", 1)], ap=X[:, :, 0:3], constant=0.0)
        xs = 0
        rs = 0
        for pr in range(4):
            for g in range(2):
                self.dma("pool", self.gw[:, g, :].rearrange("p (t n) -> p t n", t=13), self.D("gwr")[j, g, pr], [], [("gw", g)])
            for ch in range(5):
                gc = pr * 5 + ch
                sl = xs
                xs ^= 1

                def evx(ti, c0, c1, ps, bk, sl=sl):
                    self.I("act", "activation", [("ps", bk)], [("X", sl)], out=X[:, sl, 3 + c0:3 + c1], in_=ps[:, 0:c1 - c0], func=AF.Copy)
                self.proj(self.D("winr")[j, 20 + gc], KC, self.uT, "uT", (0, 1, 2), evx)
                ck = [("C", ch)]
                self.I("dve", "tensor_scalar", [("X", sl), "rgs"], ck, out=C[:, ch, :], in0=X[:, sl, 0:TT],
                       scalar1=self.rgs[:, 0, gc:gc + 1], scalar2=self.rgs[:, 4, gc:gc + 1], op0=ALU.mult, op1=ALU.add)
                for tap in range(1, 4):
                    self.I("dve", "scalar_tensor_tensor", [("X", sl), "rgs"] + ck, ck, out=C[:, ch, :], in0=X[:, sl, tap:tap + TT],
                           scalar=self.rgs[:, tap, gc:gc + 1], in1=C[:, ch, :], op0=ALU.mult, op1=ALU.add)
                self.I("act", "activation", ck, [("Cb", ch)], out=Cb[:, ch, :], in_=C[:, ch, :], func=AF.Copy)
            for jc in range(5):
                gc = pr * 5 + jc
                sl = rs
                rs ^= 1
                ins = [ic for ic in range(5) if (ic, jc) in GT_IDX]
                for g, (dst, dk) in enumerate(((R1, "R1"), (I1, "I1"))):
                    banks = [self.bank() for _ in range(3)]
                    mms = []
                    for n, ic in enumerate(ins):
                        for ti, (c0, c1) in enumerate(TILES):
                            mms.append((self.psb[banks[ti]][:, 0:c1 - c0], self.gw[:, g, GT_IDX[(ic, jc)] * 128:(GT_IDX[(ic, jc)] + 1) * 128],
                                        Cb[:, ic, c0:c1], n == 0, n == len(ins) - 1))
                    self.MM([("gw", g)] + [("Cb", ic) for ic in ins], [("ps", b) for b in banks], mms)
                    for ti, (c0, c1) in enumerate(TILES):
                        kw = {}
                        wr = [(dk, sl)]
                        if g == 0 and ti > 0:
                            kw["accum_out"] = self.RS[:, gc, ti - 1:ti]
                            wr = wr + [("RS", gc)]
                        self.I("act", "activation", [("ps", banks[ti]), "rgs"], wr, out=dst[:, sl, c0:c1], in_=self.psb[banks[ti]][:, 0:c1 - c0],
                               func=AF.Sigmoid, bias=self.rgs[:, 5 + g, gc:gc + 1], **kw)
                self.I("act", "activation", [("R1", sl), "cneg"], [("R1", sl)], out=R1[:, sl, :], in_=R1[:, sl, :], func=AF.Exp, scale=self.cneg[:, gc:gc + 1])
                self.I("dve", "tensor_tensor", [("R1", sl)], [("T1", sl)], out=T1[:, sl, :], in0=R1[:, sl, :], in1=R1[:, sl, :], op=ALU.mult)
                self.I("act", "activation", [("T1", sl), "ones_f"], [("T1", sl)], out=T1[:, sl, :], in_=T1[:, sl, :], func=AF.Sqrt, scale=-1.0, bias=one)
                self.I("dve", "tensor_tensor", [("I1", sl), ("C", jc)], [("I1", sl)], out=I1[:, sl, :], in0=I1[:, sl, :], in1=C[:, jc, :], op=ALU.mult)
                self.I("dve", "tensor_tensor", [("I1", sl), ("T1", sl)], [("I1", sl)], out=I1[:, sl, :], in0=I1[:, sl, :], in1=T1[:, sl, :], op=ALU.mult)
                self.I("dve", "tensor_tensor_scan", [("R1", sl), ("I1", sl)], [("T1", sl)], out=T1[:, sl, 0:16], data0=R1[:, sl, 0:16], data1=I1[:, sl, 0:16],
                       initial=0.0, op0=ALU.mult, op1=ALU.add)
                self.I("dve", "tensor_tensor_scan", [("R1", sl), ("I1", sl)], [("T1", sl)], out=T1[:, sl, C_OWN:TT], data0=R1[:, sl, C_OWN:TT], data1=I1[:, sl, C_OWN:TT],
                       initial=0.0, op0=ALU.mult, op1=ALU.add)
                self.I("dve", "tensor_copy", [("T1", sl)], [("SM", gc)], out=self.SM[:, gc:gc + 1], in_=T1[:, sl, 15:16])
                self.I("dve", "tensor_copy", [("T1", sl)], [("ST", gc)], out=self.ST[:, 20 + gc:21 + gc], in_=T1[:, sl, TT - 1:TT])
                self.dma("sp", self.ab_d[0, gc], R1[:, sl, :], [("R1", sl)], [("abd", 0, gc)], key=("abst", 0, sl))
                self.dma("sp", self.ab_d[1, gc], I1[:, sl, :], [("I1", sl)], [("abd", 1, gc)], key=("abst", 1, sl))
        allrs = [("RS", c) for c in range(RC)]
        allst = [("ST", c) for c in range(RC)]
        self.I("dve", "tensor_tensor", allrs, ["tmp20"], out=self.tmp20[:, :], in0=self.RS[:, :, 0], in1=self.RS[:, :, 1], op=ALU.add)
        self.I("dve", "tensor_tensor", ["tmp20", "cneg"], ["tmp20"], out=self.tmp20[:, :], in0=self.tmp20[:, :], in1=self.cneg[:, :], op=ALU.mult)
        self.I("act", "activation", ["tmp20"] + allst, allst, out=self.ST[:, 0:20], in_=self.tmp20[:, :], func=AF.Exp)
        self.dma("sp", self.cc_in.ap(), self.ST[:, :], allst, ["cc_in"])
        self.coll(self.cc_in, self.cc_out, ["cc_in"], ["cc_out"], "cc_c")
        self.dma("sp", self.G4[:, :, :], self.cc_out.ap().rearrange("(r p) c -> p r c", r=4), ["cc_out"], ["G4"])
        allsm = [("SM", c) for c in range(RC)]
        self.I("dve", "tensor_copy", allsm, ["HIN"], out=self.HIN[:, :], in_=self.SM[:, :])
        for r in range(3):
            self.I("dve", "tensor_tensor", ["G4", "HIN"], ["tmp20"], out=self.tmp20[:, :], in0=self.G4[:, r, 0:20], in1=self.HIN[:, :], op=ALU.mult)
            self.I("dve", "tensor_tensor", ["G4", "tmp20"], ["tmp20"], out=self.tmp20[:, :], in0=self.tmp20[:, :], in1=self.G4[:, r, 20:40], op=ALU.add)
            self.I("dve", "tensor_tensor", ["HIN", "tmp20"], ["tmp20"], out=self.tmp20[:, :], in0=self.tmp20[:, :], in1=self.HIN[:, :], op=ALU.subtract)
            self.I("dve", "scalar_tensor_tensor", ["HIN", "tmp20", "pcs"], ["HIN"], out=self.HIN[:, :], in0=self.tmp20[:, :], scalar=self.pcs[:, r:r + 1],
                   in1=self.HIN[:, :], op0=ALU.mult, op1=ALU.add)
        self.fence_all()
        A2r = ar[:, 0:2086].rearrange("p (s t) -> p s t", s=2)
        B2r = ar[:, 2086:4172].rearrange("p (s t) -> p s t", s=2)
        HS = ar[:, 4172:5215].rearrange("p (s t) -> p s t", s=1)
        GG = ar[:, 5215:6258].rearrange("p (s t) -> p s t", s=1)
        Y = ar[:, 6258:16688].bitcast(BF16).rearrange("p (c t) -> p c t", c=RC)
        self.I("dve", "memset", [], [("HS", 0)], ap=HS[:, :, :], constant=0.0)
        for gc in range(RC):
            sl = 0
            ab = gc % 2
            A2 = A2r[:, ab, :]
            B2 = B2r[:, ab, :]
            self.dma("sp", A2, self.ab_d[0, gc], [("abd", 0, gc)], [("A2", ab)])
            self.dma("sp", B2, self.ab_d[1, gc], [("abd", 1, gc)], [("B2", ab)])
            self.I("dve", "tensor_tensor_scan", [("A2", ab), ("B2", ab)], [("HS", sl)], out=HS[:, sl, 0:16], data0=A2[:, 0:16], data1=B2[:, 0:16],
                   initial=0.0, op0=ALU.mult, op1=ALU.add)
            self.I("dve", "tensor_tensor_scan", [("A2", ab), ("B2", ab), "HIN"], [("HS", sl)], out=HS[:, sl, C_OWN:TT], data0=A2[:, C_OWN:TT], data1=B2[:, C_OWN:TT],
                   initial=self.HIN[:, gc:gc + 1], op0=ALU.mult, op1=ALU.add)

            def evg(ti, c0, c1, ps, bk, sl=sl, gc=gc):
                self.I("act", "activation", [("ps", bk)], [("GG", sl, ti)], out=GG[:, sl, c0:c1], in_=ps[:, 0:c1 - c0], func=GELU)
                self.I("dve", "tensor_tensor", [("GG", sl, ti), ("HS", sl)], [("Y", gc, ti)], out=Y[:, gc, c0:c1], in0=GG[:, sl, c0:c1], in1=HS[:, sl, c0:c1], op=ALU.mult)
            self.proj(self.D("winr")[j, gc], KC, self.uT, "uT", (0, 1, 2), evg)
        for oc in range(KC):
            def evo(ti, c0, c1, ps, bk, oc=oc):
                self.I("dve", "tensor_tensor", [("ps", bk), ("hT", oc, ti)], [("hT", oc, ti)],
                       out=self.hT[:, oc, c0:c1], in0=ps[:, 0:c1 - c0], in1=self.hT[:, oc, c0:c1], op=ALU.add)
            self.proj(self.D("woutr")[j, oc], RC, Y, "Y", (0, 1, 2), evo)
        self.fence_all()

    def rsqrt_ps(self, ps, bk, w, c0, c1, ti, eps):
        ec = self.epsc[:, 0:1] if eps == 1e-6 else self.epsc[:, 1:2]
        self.I("act", "activation", [("ps", bk), "epsc"], [("rstd", ti)], out=self.rstd[:, c0:c1], in_=ps[:, 0:w], func=AF.Ln, bias=ec)
        self.I("act", "activation", [("rstd", ti)], [("rstd", ti)], out=self.rstd[:, c0:c1], in_=self.rstd[:, c0:c1], func=AF.Exp, scale=-0.5)

    def sumsq_rstd(self, ti, eps=1e-6):
        c0, c1 = TILES[ti]
        w = c1 - c0
        bk = self.bank()
        ps = self.psb[bk]
        for c in range(KC):
            sl = self.sqr_i
            self.sqr_i = (self.sqr_i + 1) % 4
            self.I("act", "activation", [("hT", c, ti)], [("sqr", sl)], out=self.sqr[:, sl, 0:w], in_=self.hT[:, c, c0:c1], func=AF.Square)
            self.MM([("sqr", sl), "ones_bf"], [("ps", bk)], [(ps[:, 0:w], self.ones_bf[:, :], self.sqr[:, sl, 0:w], c == 0, c == KC - 1)])
        self.rsqrt_ps(ps, bk, w, c0, c1, ti, eps)

    def rmsnorm(self, gi, tiles=(0, 1, 2)):
        for ti in tiles:
            c0, c1 = TILES[ti]
            self.sumsq_rstd(ti)
            for c in range(KC):
                self.I("dve", "scalar_tensor_tensor", [("hT", c, ti), ("rstd", ti), "gains"], [("uT", c, ti)],
                       out=self.uT[:, c, c0:c1], in0=self.hT[:, c, c0:c1], scalar=self.gains[:, gi, c:c + 1],
                       in1=self.rstd[:, c0:c1], op0=ALU.mult, op1=ALU.mult)

    def proj(self, wsrc, nkc, src, srckey, tiles, evac):
        s, wt = self.wload(wsrc, nkc)
        banks = {ti: self.bank() for ti in tiles}
        mms = []
        for k in range(nkc):
            for ti in tiles:
                c0, c1 = TILES[ti]
                mms.append((self.psb[banks[ti]][:, 0:c1 - c0], wt[:, k, :], src[:, k, c0:c1], k == 0, k == nkc - 1))
        self.MM([("w", s)] + [(srckey, k, ti) for k in range(nkc) for ti in tiles], [("ps", banks[ti]) for ti in tiles], mms)
        for ti in tiles:
            c0, c1 = TILES[ti]
            evac(ti, c0, c1, self.psb[banks[ti]], banks[ti])

    CS0 = 14614

    def rope_tables(self):
        ar = self.arena
        self.cosT = ar[:, self.CS0:self.CS0 + TT]
        self.sinT = ar[:, self.CS0 + TT:self.CS0 + 2 * TT]
        self.Lown = ar[:, 14300:14556].bitcast(BF16).rearrange("p (r n) -> p r n", r=4)
        self.dma("sp", self.cosT, self.D("cosd")[:, :], [], ["cosT"])
        self.dma("sp", self.sinT, self.D("sind")[:, :], [], ["sinT"])
        self.dma("sp", self.rot_sb[:, :], self.D("rot")[:, :], [], ["rot"])
        self.dma("pool", self.band_bf[:, :, :], self.D("band")[:, :, :], [], ["band"])
        self.dma("sp", self.ones_f[:, :], self.D("ident")[:, :], ["ones_f"], ["ones_f"])
        self.I("dve", "tensor_copy", ["ones_f"], [("Lown", 0)], out=self.Lown[:, 0, :], in_=self.ones_f[:, :])
        self.I("dve", "memset", [("Lown", 0)], ["ones_f"], ap=self.ones_f[:, :], constant=1.0)

    def rope_evac(self, ps, bk, c0, c1, xq, xk, t1, t1k, t2, t2k, dst, dkeys):
        w = c1 - c0
        self.I("act", "activation", [("ps", bk)], [xk], out=xq[:, 0:w], in_=ps[:, 0:w], func=AF.Copy)
        b2 = self.bank()
        self.MM([xk, "rot"], [("ps", b2)], [(self.psb[b2][:, 0:w], self.rot_sb[:, :], xq[:, 0:w], True, True)])
        self.I("dve", "tensor_tensor", [xk, "cosT"], [t1k], out=t1[:, 0:w], in0=xq[:, 0:w], in1=self.cosT[:, c0:c1], op=ALU.mult)
        self.I("dve", "tensor_tensor", [("ps", b2), "sinT"], [t2k], out=t2[:, 0:w], in0=self.psb[b2][:, 0:w], in1=self.sinT[:, c0:c1], op=ALU.mult)
        self.I("dve", "tensor_tensor", [t1k, t2k], dkeys, out=dst, in0=t1[:, 0:w], in1=t2[:, 0:w], op=ALU.add)

    def kv(self):
        ar = self.arena
        self.fence_all()
        self.rope_tables()
        self.rmsnorm(6)
        F = ar[:, 2 * TT:2 * TT + 6 * 512].rearrange("p (s t) -> p s t", s=6)
        o = 2 * TT + 6 * 512
        kto = ar[:, o:o + 1024].bitcast(BF16).rearrange("p (s t) -> p s t", s=2)
        o += 1024
        vo = ar[:, o:o + 512].bitcast(BF16).rearrange("p (s t) -> p s t", s=2)
        vs = 0
        for (chunks) in ([0, 1, 2, 3, 8], [4, 5, 6, 7]):
            for cb in range(4):
                banks = {tc: self.bank() for tc in chunks}
                for kq in range(4):
                    s_, wt = self.wload(self.D("wvr")[cb, :, kq * 4:(kq + 1) * 4, :], 4, 512)
                    mms = []
                    for k4 in range(4):
                        kc = kq * 4 + k4
                        for tc in chunks:
                            if tc == 8:
                                lcols = (0, 16)
                            else:
                                lcols = (C_OWN + tc * 128, C_OWN + (tc + 1) * 128)
                            m = lcols[1] - lcols[0]
                            mms.append((self.psb[banks[tc]][0:m, :], self.uT[:, kc, lcols[0]:lcols[1]], wt[:, k4, :], kc == 0, kc == KC - 1))
                    self.MM([("w", s_)] + [("uT", kc, t) for kc in range(kq * 4, kq * 4 + 4) for t in range(3)], [("ps", banks[tc]) for tc in chunks], mms)
                for tc in chunks:
                    sl = vs
                    vs ^= 1
                    m = 16 if tc == 8 else 128
                    self.I("act", "activation", [("ps", banks[tc])], [("vo", sl)], out=vo[0:m, sl, :], in_=self.psb[banks[tc]][0:m, :], func=AF.Copy)
                    if tc == 8:
                        self.dma("sp", self.vm_d.ap()[:, cb * 512:(cb + 1) * 512], vo[0:16, sl, :], [("vo", sl)], [("vmd", cb)], key=("vst", sl))
                    else:
                        for hh in range(2):
                            self.dma("sp", self.v_loc[cb * 2 + hh].ap()[tc * 128:(tc + 1) * 128, :], vo[:, sl, hh * 256:(hh + 1) * 256], [("vo", sl)],
                                     [("vl", tc, cb, hh)], key=("vst", sl, hh))
        for h in range(8):
            self.coll(self.v_loc[h], self.v_all[h], [("vl", tc, h // 2, h % 2) for tc in range(8)], [("v_all", h)], ("cc_v", h))
        fi = 0
        for oc in range(16):
            ks = oc % 2

            def evk(ti, c0, c1, ps, bk, oc=oc, ks=ks):
                nonlocal fi
                a, b, c = fi % 6, (fi + 1) % 6, (fi + 2) % 6
                fi += 3
                if ti == 0:
                    dst = self.ktm[:, oc, :]
                    dk = [("ktm", oc)]
                    cc0, cc1 = 0, 16
                else:
                    dst = kto[:, ks, c0 - C_OWN:c1 - C_OWN]
                    dk = [("kto", ks, ti)]
                    cc0, cc1 = c0, c1
                self.rope_evac(ps, bk, cc0, cc1, F[:, a, :], ("F", a), F[:, b, :], ("F", b), F[:, c, :], ("F", c), dst, dk)
            self.proj(self.D("wkr")[oc], KC, self.uT, "uT", (0, 1, 2), evk)
            self.dma("sp", self.kt_loc[oc // 2].ap()[(oc % 2) * 128:(oc % 2 + 1) * 128, :], kto[:, ks, :], [("kto", ks, 1), ("kto", ks, 2)], [("ktl", oc)], key=("kst", ks))
            if oc % 2 == 1:
                self.coll(self.kt_loc[oc // 2], self.kt_all[oc // 2], [("ktl", oc - 1), ("ktl", oc)], [("kt_all", oc // 2)], ("cc_k", oc // 2))
        self.fence_all()

    def attn(self, j):
        layer = 2 + j
        lam_init = 0.8 - 0.6 * math.exp(-0.3 * layer)
        ar = self.arena
        self.fence_all()
        self.rmsnorm(7 + j, tiles=(1, 2))
        KTh = ar[:, 0:4096].bitcast(BF16).rearrange("p (c r t) -> p c r t", c=2, r=4)
        Vh = ar[:, 4096:8192 + 128].bitcast(BF16).rearrange("p (k e) -> p k e", e=256)
        o = 8192 + 128
        QTh = ar[:, o:o + 1024].bitcast(BF16).rearrange("p (c t) -> p c t", c=2)
        o += 1024
        OTh = ar[:, o:o + 1024].bitcast(BF16).rearrange("p (c t) -> p c t", c=2)
        o += 1024
        Pt = ar[:, o:o + 768].bitcast(BF16).rearrange("p (s t) -> p s t", s=3)
        o += 768
        F = ar[:, o:o + 3072].rearrange("p (s t) -> p s t", s=6)
        o += 3072
        sqb = self.sqr
        assert o <= 14300, o
        self.dma("sp", self.lamt[:, 0:4], self.D("blam")[:, j, :], [], ["lamt"])
        self.dma("sp", self.sgc[:, :], self.D("sg")[:, j, :], [], ["sgc"])
        self.I("dve", "tensor_tensor", ["lamt"], ["lamt2"], out=self.lamt[:, 4:5], in0=self.lamt[:, 0:1], in1=self.lamt[:, 1:2], op=ALU.mult)
        self.I("dve", "tensor_tensor", ["lamt", "lamt2"], ["lamt2"], out=self.lamt[:, 5:6], in0=self.lamt[:, 2:3], in1=self.lamt[:, 3:4], op=ALU.mult)
        bl = self.bank()
        self.MM(["lamt2", "ones_f"], [("ps", bl)], [(self.psb[bl][:, 0:2], self.ones_f[:, :], self.lamt[:, 4:6], True, True)])
        self.I("act", "activation", [("ps", bl)], ["lamt3"], out=self.lamt[:, 6:8], in_=self.psb[bl][:, 0:2], func=AF.Exp)
        self.I("dve", "tensor_tensor", ["lamt3"], ["neglam"], out=self.lamt[:, 4:5], in0=self.lamt[:, 7:8], in1=self.lamt[:, 6:7], op=ALU.subtract)
        self.I("dve", "tensor_scalar", ["neglam"], ["neglam"], out=self.lamt[:, 4:5], in0=self.lamt[:, 4:5], scalar1=-lam_init, scalar2=None, op0=ALU.add)
        self.I("dve", "tensor_scalar", ["sgc"], ["sgc"], out=self.sgc[:, :], in0=self.sgc[:, :], scalar1=1.0 - lam_init, scalar2=None, op0=ALU.mult)
        neglam = self.lamt[:, 4:5]
        pi = 0
        fi = 0
        for hd in range(8):
            for cp in range(2):
                for r in range(3):
                    row = r * 256 + cp * 128
                    self.dma("sp", KTh[:, cp, r, :], self.kt_all[hd].ap()[row:row + 128, :], [("kt_all", hd)], [("KTh", cp, r)])
                self.dma("sp", KTh[:, cp, 3, :], self.kt_loc[hd].ap()[cp * 128:(cp + 1) * 128, :], [("ktl", hd * 2 + cp)], [("KTh", cp, 3)])
            for r in range(3):
                self.dma("sp", Vh[:, r * 8:(r + 1) * 8, :], self.v_all[hd].ap()[r * 1024:(r + 1) * 1024, :].rearrange("(c p) e -> p c e", p=128),
                         [("v_all", hd)], [("Vh", r)])
            self.dma("sp", Vh[:, 24:32, :], self.v_loc[hd].ap().rearrange("(c p) e -> p c e", p=128),
                     [("vl", tc, hd // 2, hd % 2) for tc in range(8)], [("Vh", 3)])
            self.dma("sp", Vh[0:16, 32, :], self.vm_d.ap()[:, hd * 256:(hd + 1) * 256], [("vmd", cb) for cb in range(4)], [("Vh", 4)])
            for cp in range(2):
                oc = hd * 2 + cp

                def evq(ti, c0, c1, ps, bk, cp=cp):
                    nonlocal fi
                    a, b, c = fi % 6, (fi + 1) % 6, (fi + 2) % 6
                    fi += 3
                    self.rope_evac(ps, bk, c0, c1, F[:, a, :], ("F", a), F[:, b, :], ("F", b), F[:, c, :], ("F", c),
                                   QTh[:, cp, c0 - C_OWN:c1 - C_OWN], [("QTh", cp, ti)])
                self.proj(self.D("wqr")[j, oc], KC, self.uT, "uT", (1, 2), evq)
            pending = None
            for qt in range(2):
                q0 = qt * 512
                accs = {0: (2, 3, 4), 1: (5, 6, 7)}
                items = [("m", 0, 0)] + [("r", r, c) for r in range(3) for c in range(8)] + [("o", 3, c) for c in range(4 * qt + 4)]
                flat = [(cp, n, it) for cp in range(2) for n, it in enumerate(items)]

                def info(cp, kind, r, c):
                    if kind == "m":
                        return (16, self.ktm[:, hd * 2 + cp, :], [("ktm", hd * 2 + cp)], 32, [("Vh", 4)], None, None)
                    lk = KTh[:, cp, r, c * 128:(c + 1) * 128]
                    if kind == "r":
                        bias, band = self.pcs[:, 12 + r:13 + r], None
                    else:
                        bias, band = None, (None if c < 4 * qt else c - 4 * qt)
                    return (128, lk, [("KTh", cp, r)], r * 8 + c, [("Vh", r)], bias, band)

                def emit_S(i):
                    cp, n, (kind, r, c) = flat[i]
                    nk, lk, lkk, vidx, vk, bias, band = info(cp, kind, r, c)
                    sb_ = i % 2
                    if band is None:
                        self.MM(lkk + [("QTh", cp, 1 + qt)], [("ps", sb_)], [(self.psb[sb_][0:nk, :], lk, QTh[:, cp, q0:q0 + 512], True, True)])
                    else:
                        self.MM(lkk + [("QTh", cp, 1 + qt), ("Lown", 0), "band"], [("ps", sb_)],
                                [(self.psb[sb_][0:nk, :], lk, QTh[:, cp, q0:q0 + 512], True, False),
                                 (self.psb[sb_][0:nk, 0:128 * (band + 1)], self.Lown[:, 0, :], self.band_bf[:, band, 0:128 * (band + 1)], False, True)])

                def emit_rest(i):
                    nonlocal pi
                    cp, n, (kind, r, c) = flat[i]
                    nk, lk, lkk, vidx, vk, bias, band = info(cp, kind, r, c)
                    bO0, bO1, bS = accs[cp]
                    first, lastk = n == 0, n == len(items) - 1
                    sb_ = i % 2
                    ps = self.psb[sb_]
                    sl = pi % 3
                    pi += 1
                    kw = {} if bias is None else {"bias": bias[0:nk, :]}
                    self.I("act", "activation", [("ps", sb_), "pcs"], [("Pt", sl)], out=Pt[0:nk, sl, :], in_=ps[0:nk, :], func=AF.Exp, scale=SCALE, **kw)
                    self.MM(vk + [("Pt", sl), "ones1"], [("ps", bO0), ("ps", bO1), ("ps", bS)],
                            [(self.psb[bO0][:, :], Vh[0:nk, vidx, 0:128], Pt[0:nk, sl, :], first, lastk),
                             (self.psb[bO1][:, :], Vh[0:nk, vidx, 128:256], Pt[0:nk, sl, :], first, lastk),
                             (self.psb[bS][:, :], self.ones1[0:nk, :], Pt[0:nk, sl, :], first, lastk)])

                emit_S(0)
                for i in range(len(flat)):
                    if i + 1 < len(flat):
                        emit_S(i + 1)
                    emit_rest(i)
                    if i == 10 and pending is not None:
                        pending(i % 2)
                        pending = None
                rc0, rc1, os0, os1, t0, t1 = F[:, 4, :], F[:, 5, :], F[:, 2, :], F[:, 3, :], F[:, 0, :], F[:, 1, :]
                self.I("dve", "reciprocal", [("ps", accs[0][2])], [("F", 4)], out=rc0, in_=self.psb[accs[0][2]][:, :])
                self.I("act", "activation", [("ps", accs[1][0])], [("F", 0)], out=t0, in_=self.psb[accs[1][0]][:, :], func=AF.Copy)
                self.I("dve", "reciprocal", [("ps", accs[1][2])], [("F", 5)], out=rc1, in_=self.psb[accs[1][2]][:, :])
                self.I("act", "activation", [("ps", accs[1][1])], [("F", 1)], out=t1, in_=self.psb[accs[1][1]][:, :], func=AF.Copy)
                self.I("dve", "tensor_tensor", [("ps", accs[0][0]), ("F", 4)], [("F", 2)], out=os0, in0=self.psb[accs[0][0]][:, :], in1=rc0, op=ALU.mult)
                self.I("dve", "tensor_tensor", [("ps", accs[0][1]), ("F", 4)], [("F", 3)], out=os1, in0=self.psb[accs[0][1]][:, :], in1=rc0, op=ALU.mult)
                def part2(bn, qt=qt, q0=q0):
                    self.I("dve", "tensor_scalar", [("F", 5), "neglam"], [("F", 5)], out=rc1, in0=rc1, scalar1=neglam, scalar2=None, op0=ALU.mult)
                    for ec, (osb, fk, tt, tk) in enumerate(((os0, 2, t0, 0), (os1, 3, t1, 1))):
                        self.I("dve", "tensor_tensor", [("F", tk), ("F", 5)], [("F", tk)], out=tt, in0=tt, in1=rc1, op=ALU.mult)
                        self.I("dve", "tensor_tensor", [("F", fk), ("F", tk)], [("F", fk)], out=osb, in0=osb, in1=tt, op=ALU.add)
                        self.I("act", "activation", [("F", fk)], [("sqr", ec)], out=sqb[:, ec, :], in_=osb, func=AF.Square)
                    self.MM([("sqr", 0), ("sqr", 1), "ones256"], [("ps", bn)], [(self.psb[bn][:, :], self.ones256[:, :], sqb[:, 0, :], True, False),
                                                                                  (self.psb[bn][:, :], self.ones256[:, :], sqb[:, 1, :], False, True)])
                    self.rsqrt_ps(self.psb[bn], bn, 512, 19, 531, 1, 1e-5)
                    for ec, (osb, fk) in enumerate(((os0, 2), (os1, 3))):
                        self.I("dve", "scalar_tensor_tensor", [("F", fk), ("rstd", 1), "sgc"], [("OTh", ec, qt)], out=OTh[:, ec, q0:q0 + 512], in0=osb,
                               scalar=self.sgc[:, ec:ec + 1], in1=self.rstd[:, 19:531], op0=ALU.mult, op1=ALU.mult)
                pending = part2
            if pending is not None:
                pending(self.bank())
                pending = None
            ws = [self.wload(self.D("wor2")[j, hd, kc2], 1, 2048) for kc2 in range(2)]
            for oc in range(KC):
                banks = [self.bank(), self.bank()]
                mms = []
                for kc2 in range(2):
                    for qt in range(2):
                        mms.append((self.psb[banks[qt]][:, :], ws[kc2][1][:, 0, oc * 128:(oc + 1) * 128], OTh[:, kc2, qt * 512:(qt + 1) * 512], kc2 == 0, kc2 == 1))
                self.MM([("w", ws[0][0]), ("w", ws[1][0])] + [("OTh", e, q) for e in range(2) for q in range(2)], [("ps", b) for b in banks], mms)
                for qt in range(2):
                    c0, c1 = TILES[1 + qt]
                    self.I("dve", "tensor_tensor", [("ps", banks[qt]), ("hT", oc, 1 + qt)], [("hT", oc, 1 + qt)],
                           out=self.hT[:, oc, c0:c1], in0=self.psb[banks[qt]][:, :], in1=self.hT[:, oc, c0:c1], op=ALU.add)
        self.fence_all()


    def mlp(self, layer, tiles=(0, 1, 2)):
        self.rmsnorm(2 + layer, tiles)
        hid = self.arena[:, 0:KC * TT // 2].bitcast(BF16).rearrange("p (c t) -> p c t", c=KC)
        off = KC * TT // 2
        sqt = self.arena[:, off:off + 3 * 512].rearrange("p (s t) -> p s t", s=3)
        for grp in range(4):
            for oc in range(16):
                def evac(ti, c0, c1, ps, bk, oc=oc):
                    w = c1 - c0
                    sl = self.sq_i
                    self.sq_i = (self.sq_i + 1) % 3
                    self.I("act", "activation", [("ps", bk)], [("sqt", sl)], out=sqt[:, sl, 0:w], in_=ps[:, 0:w], func=AF.Square)
                    self.I("dve", "scalar_tensor_tensor", [("ps", bk), ("sqt", sl)], [("hid", oc, ti)],
                           out=hid[:, oc, c0:c1], in0=ps[:, 0:w], scalar=0.0, in1=sqt[:, sl, 0:w], op0=ALU.is_gt, op1=ALU.mult)
                self.proj(self.D("w1r")[layer, grp * 16 + oc], KC, self.uT, "uT", tiles, evac)
            for oc2 in range(16):
                def evac2(ti, c0, c1, ps, bk, oc2=oc2):
                    w = c1 - c0
                    self.I("dve", "tensor_tensor", [("ps", bk), ("hT", oc2, ti)], [("hT", oc2, ti)],
                           out=self.hT[:, oc2, c0:c1], in0=ps[:, 0:w], in1=self.hT[:, oc2, c0:c1], op=ALU.add)
                self.proj(self.D("w2r")[layer, grp, oc2], KC, hid, "hid", tiles, evac2)

    def final(self):
        gi = 9
        for ti in (1, 2):
            c0, c1 = TILES[ti]
            self.sumsq_rstd(ti)
            for c in range(KC):
                self.I("dve", "scalar_tensor_tensor", [("hT", c, ti), ("rstd", ti), "gains"], [("hT", c, ti)],
                       out=self.hT[:, c, c0:c1], in0=self.hT[:, c, c0:c1], scalar=self.gains[:, gi, c:c + 1],
                       in1=self.rstd[:, c0:c1], op0=ALU.mult, op1=ALU.mult)
        for c in range(KC):
            self.dma("sp", self.outd[:, c, :], self.hT[:, c, C_OWN:TT], [("hT", c, 1), ("hT", c, 2)], [("outd", c)], key="outst")

    def store_h(self):
        for c in range(KC):
            self.dma("sp", self.houtd[:, c, :], self.hT[:, c, :], [("hT", c, 0), ("hT", c, 1), ("hT", c, 2)], [("houtd", c)], key="outst")

    def finish(self):
        P = self.P
        outs = [i for i, op in enumerate(P.ops) if op["dma"] == "outst" or op["dma"] == "kvst"]
        idx = P.add("sp", None, [], [])
        P.ops[idx]["deps"].update(outs)

    def build(self):
        self.setup()
        for st in self.stages:
            if st.startswith("mlp"):
                L = int(st[3:])
                self.mlp(L, (0, 1, 2) if L < 2 else (1, 2))
            elif st.startswith("rg"):
                self.rglru(int(st[2:]))
            elif st == "halo":
                self.halo_exchange()
            elif st == "kv":
                self.kv()
            elif st.startswith("at"):
                self.attn(int(st[2:]))
        if self.last:
            self.final()
        else:
            self.store_h()
        self.finish()
        self.P.emit(self.nc, self.es)
        self.es.close()
        return self.nc


def _fm(v, nch):
    return np.ascontiguousarray(v.reshape(nch, 128).T)


def _wtiles(w, kc, noc):
    return np.ascontiguousarray(w.reshape(kc, 128, noc, 128).transpose(2, 1, 0, 3))


def prep_shared(inp):
    f = np.float32
    sh = {}
    w1 = inp["mlp_w1"]
    sh["w1r"] = np.stack([_wtiles(w1[l], KC, 64) for l in range(4)])
    w2 = inp["mlp_w2"]
    sh["w2r"] = np.stack([np.stack([_wtiles(w2[l, g * 2048:(g + 1) * 2048], KC, 16) for g in range(4)]) for l in range(4)])
    sh["winr"] = np.stack([_wtiles(inp["a_w_in"][j], KC, 40) for j in range(2)])
    sh["woutr"] = np.stack([_wtiles(inp["a_w_out"][j], RC, 16) for j in range(2)])
    gw = np.zeros((2, 2, 4, 128, 13, 128), f)
    for j in range(2):
        for g, key in enumerate(("a_w_r", "a_w_i")):
            full = np.zeros((DRNN, DRNN), f)
            for n in range(16):
                full[n * 160:(n + 1) * 160, n * 160:(n + 1) * 160] = inp[key][j, n]
            for pr in range(4):
                for ti, (ic, jc) in enumerate(GT):
                    r0 = pr * 640 + ic * 128
                    c0 = pr * 640 + jc * 128
                    gw[j, g, pr, :, ti, :] = full[r0:r0 + 128, c0:c0 + 128]
    sh["gwr"] = gw
    wkv = inp["w_kv"]
    sh["wkr"] = _wtiles(wkv[:, :2048], KC, 16)
    sh["wvr"] = np.ascontiguousarray(wkv[:, 2048:].reshape(KC, 128, 4, 512).transpose(2, 1, 0, 3))
    sh["wqr"] = np.stack([_wtiles(inp["b_w_q"][j], KC, 16) for j in range(2)])
    sh["wor2"] = np.ascontiguousarray(inp["b_w_o"].reshape(2, 8, 2, 128, 2048))
    gains = np.zeros((128, 10, KC), f)
    gl = [inp["a_norm_g"][0], inp["a_norm_g"][1], inp["mlp_norm_g"][0], inp["mlp_norm_g"][1], inp["mlp_norm_g"][2],
          inp["mlp_norm_g"][3], inp["kv_norm_g"], inp["b_norm_g"][0], inp["b_norm_g"][1], inp["final_norm_g"]]
    for i, g in enumerate(gl):
        gains[:, i, :] = _fm(g, KC)
    sh["gains"] = gains
    rgv = np.zeros((2, 128, 8, RC), f)
    for j in range(2):
        for t in range(4):
            rgv[j, :, t, :] = _fm(inp["a_conv_w"][j, t], RC)
        rgv[j, :, 4, :] = _fm(inp["a_conv_b"][j], RC)
        rgv[j, :, 5, :] = _fm(inp["a_b_r"][j], RC)
        rgv[j, :, 6, :] = _fm(inp["a_b_i"][j], RC)
        rgv[j, :, 7, :] = _fm(inp["a_lambda"][j], RC)
    sh["rgv"] = rgv
    sh["blam"] = np.ascontiguousarray(inp["b_lambda"].transpose(2, 0, 1)).astype(f)
    sh["sg"] = np.ascontiguousarray(inp["b_subln_g"].reshape(2, 2, 128).transpose(2, 0, 1)).astype(f)
    rot = np.zeros((128, 128), f)
    for d in range(64):
        rot[d + 64, d] = -1.0
        rot[d, d + 64] = 1.0
    sh["rot"] = rot
    band = np.zeros((128, 4, 512), f)
    k = np.arange(128)[:, None]
    qy = np.arange(512)[None, :]
    for j in range(4):
        band[:, j, :] = np.where(j * 128 + k <= qy, 0.0, NEG)
    sh["band"] = band
    sh["ident"] = np.eye(128, dtype=f)
    return sh


def prep_core(inp, b, q):
    f = np.float32
    x = inp["x"]
    meta = inp["meta_tokens"]
    own = x[b, q * OWN:(q + 1) * OWN]
    halo = meta[13:16] if q == 0 else x[b, q * OWN - 3:q * OWN]
    tok = np.concatenate([meta, halo, own], 0)
    hin = np.ascontiguousarray(tok.T.reshape(KC, 128, TT).transpose(1, 0, 2))
    pc = np.zeros((128, 32), f)
    for r in range(4):
        pc[:, r] = 1.0 if r < q else 0.0
        pc[:, 4 + r] = 1.0 if r == q - 1 else 0.0
        pc[:, 8 + r] = 0.0 if r <= q else NEG
        pc[:, 12 + r] = 0.0 if r < q else NEG
        pc[:, 16 + r] = 0.0 if r == q else 1.0
        pc[:, 24 + r] = 1.0 if r == q else 0.0
    pc[:, 20] = 1.0 if q == 0 else 0.0
    pos = np.zeros((TT,), f)
    pos[0:16] = np.arange(16, dtype=f)
    pos[C_OWN:] = (NMETA + q * OWN + np.arange(OWN)).astype(f)
    inv = (1.0 / (10000.0 ** (np.arange(0, 128, 2, dtype=f) / 128.0))).astype(f)
    ang = (np.tile(inv, 2)[:, None] * pos[None, :]).astype(f)
    return {"hin": hin, "pc": pc, "cosd": np.cos(ang).astype(f), "sind": np.sin(ang).astype(f)}


_CACHE = {}


def run_stages(stages, first, last, core_maps):
    key = (tuple(stages), first, last)
    import time
    t0 = time.time()
    if key not in _CACHE:
        bld = Builder(stages, first, last)
        _CACHE[key] = (bld.build(), bld.declared)
    nc, decl = _CACHE[key]
    core_maps = [{k: v for k, v in m.items() if k in decl} for m in core_maps]
    t1 = time.time()
    r = run_bass_kernel_spmd(nc, core_maps, core_ids=list(range(8))).results
    print("build %.1fs run %.1fs" % (t1 - t0, time.time() - t1), flush=True)
    return r


def kernel(**inputs):
    inp = {k: np.asarray(v) for k, v in inputs.items()}
    sh = prep_shared(inp)
    maps = []
    for b in range(2):
        for q in range(4):
            m = dict(sh)
            m.update(prep_core(inp, b, q))
            maps.append(m)
    res = run_stages(["rg0", "mlp0", "halo", "rg1", "mlp1", "kv", "at0", "mlp2", "at1", "mlp3"], True, True, maps)
    out = np.zeros((2, 4096, D), np.float32)
    for b in range(2):
        for q in range(4):
            o = res[b * 4 + q]["out"]
            out[b, q * OWN:(q + 1) * OWN] = o.transpose(2, 1, 0).reshape(OWN, D)
    return out
```
